# Optimizing a Trainium2 kernel written in Bass

```python
import math
import jax
import jax.numpy as jnp
from jax import lax
import numpy as np

D_MODEL = 1024
BATCH = 8
SEQ = 8192
DEPTH = 4

GRID_W = 64
CTX_LEN = 256
N_MIXERS = 3
Q_BLOCK = 128
ROPE_BASE = 10000.0
NORM_EPS = 1e-6

DIFF_DH = 64
DIFF_HEADS = D_MODEL // (2 * DIFF_DH)
GQA_DH = 128
GQA_HEADS = D_MODEL // GQA_DH
GQA_KV_HEADS = max(GQA_HEADS // 4, 1)
GQA_GROUP = GQA_HEADS // GQA_KV_HEADS
GLA_HEADS = 4
GLA_DK = D_MODEL // 2
GLA_DV = D_MODEL
GLA_DK_H = GLA_DK // GLA_HEADS
GLA_DV_H = GLA_DV // GLA_HEADS
GLA_GATE_RANK = 16
GLA_TAU = 16.0
GLA_CHUNK = 64
FFN_HIDDEN = ((8 * D_MODEL // 3 + 255) // 256) * 256

N_A = len(range(0, DEPTH, N_MIXERS))
N_B = len(range(1, DEPTH, N_MIXERS))
N_C = len(range(2, DEPTH, N_MIXERS))

kernel_name = 'hybrid_diff_gqa_gla_prefix_dit'


def rmsnorm(x, g):
    x32 = x.astype(jnp.float32)
    y = x32 * lax.rsqrt(jnp.mean(x32 * x32, axis=-1, keepdims=True) + NORM_EPS)
    return (y * g.astype(jnp.float32)).astype(x.dtype)


def modulate(t, shift, scale):
    return t * (1 + scale) + shift


def swiglu(t, w_gu, w_down):
    gate, up = jnp.split(t @ w_gu, 2, axis=-1)
    return (jax.nn.silu(gate) * up) @ w_down


def axial_rope_tables(n_tokens, head_dim):
    rows = n_tokens // GRID_W
    row = jnp.broadcast_to(jnp.arange(rows)[:, None], (rows, GRID_W)).reshape(-1)
    col = jnp.broadcast_to(jnp.arange(GRID_W)[None, :], (rows, GRID_W)).reshape(-1)
    half = head_dim // 2
    inv = ROPE_BASE ** (-jnp.arange(0, half, 2, dtype=jnp.float32) / half)

    def axis_angles(pos):
        a = pos.astype(jnp.float32)[:, None] * inv[None, :]
        return jnp.concatenate([a, a], axis=-1)

    ang = jnp.concatenate([axis_angles(row), axis_angles(col)], axis=-1)
    return jnp.cos(ang), jnp.sin(ang)


def apply_axial_rope(x, cos, sin):
    q = x.shape[-1] // 4
    x32 = x.astype(jnp.float32)
    xr = x32.reshape(*x.shape[:-1], 2, 2, q)
    rot = jnp.stack([-xr[..., 1, :], xr[..., 0, :]], axis=-2).reshape(x.shape)
    return (x32 * cos + rot * sin).astype(x.dtype)


def sweep_query_blocks(fn, q):
    B, H, L, d = q.shape
    nb = L // Q_BLOCK
    qb = q.reshape(B, H, nb, Q_BLOCK, d).transpose(2, 0, 1, 3, 4)
    out = lax.map(fn, qb)
    return out.transpose(1, 2, 0, 3, 4).reshape(B, out.shape[2], L, out.shape[-1])


def diff_attention(h, hc, w_in, w_out, q_norm, k_norm, lq1, lk1, lq2, lk2, subln, lambda_init, ctx_out):
    f32 = jnp.float32
    L = h.shape[1]

    def project(t):
        Bt, Lt, _ = t.shape
        q, k, v = jnp.split(t @ w_in, 3, axis=-1)
        q = rmsnorm(q.reshape(Bt, Lt, 2 * DIFF_HEADS, DIFF_DH), q_norm).transpose(0, 2, 1, 3)
        k = rmsnorm(k.reshape(Bt, Lt, 2 * DIFF_HEADS, DIFF_DH), k_norm).transpose(0, 2, 1, 3)
        v = v.reshape(Bt, Lt, DIFF_HEADS, 2 * DIFF_DH).transpose(0, 2, 1, 3)
        return q, k, v

    q, k, v = project(h)
    qc, kc, vc = project(hc)
    cos, sin = axial_rope_tables(L, DIFF_DH)
    q = apply_axial_rope(q, cos, sin)
    k = apply_axial_rope(k, cos, sin)
    lam = (jnp.exp(jnp.sum(lq1.astype(f32) * lk1.astype(f32)))
           - jnp.exp(jnp.sum(lq2.astype(f32) * lk2.astype(f32))) + lambda_init)

    def pair(t):
        return t.reshape(t.shape[0], DIFF_HEADS, 2, t.shape[2], t.shape[3])

    def attend(qb, keys, vals):
        s = jnp.einsum('bhmtd,bhmsd->bhmts', pair(qb), keys).astype(f32) * DIFF_DH ** -0.5
        p = jax.nn.softmax(s, axis=-1)
        a = (p[:, :, 0] - lam * p[:, :, 1]).astype(vals.dtype)
        return jnp.einsum('bhts,bhse->bhte', a, vals)

    def finish(o):
        Bt, _, Lt, _ = o.shape
        o = rmsnorm(o, subln) * (1.0 - lambda_init)
        return o.transpose(0, 2, 1, 3).reshape(Bt, Lt, DIFF_HEADS * 2 * DIFF_DH) @ w_out

    keys = pair(jnp.concatenate([kc, k], axis=2))
    vals = jnp.concatenate([vc, v], axis=2)
    y = finish(sweep_query_blocks(lambda qb: attend(qb, keys, vals), q))
    yc = finish(attend(qc, pair(kc), vc)) if ctx_out else None
    return y, yc


def gqa_attention(h, hc, w_in, w_out, q_norm, k_norm, ctx_out):
    f32 = jnp.float32
    L = h.shape[1]
    kv_w = GQA_KV_HEADS * GQA_DH

    def project(t):
        Bt, Lt, _ = t.shape
        q, k, v = jnp.split(t @ w_in, [GQA_HEADS * GQA_DH, GQA_HEADS * GQA_DH + kv_w], axis=-1)
        q = rmsnorm(q.reshape(Bt, Lt, GQA_HEADS, GQA_DH), q_norm).transpose(0, 2, 1, 3)
        k = rmsnorm(k.reshape(Bt, Lt, GQA_KV_HEADS, GQA_DH), k_norm).transpose(0, 2, 1, 3)
        v = v.reshape(Bt, Lt, GQA_KV_HEADS, GQA_DH).transpose(0, 2, 1, 3)
        return q, k, v

    q, k, v = project(h)
    qc, kc, vc = project(hc)
    cos, sin = axial_rope_tables(L, GQA_DH)
    q = apply_axial_rope(q, cos, sin)
    k = apply_axial_rope(k, cos, sin)

    def attend(qb, keys, vals):
        Bq, _, T, _ = qb.shape
        qg = qb.reshape(Bq, GQA_KV_HEADS, GQA_GROUP, T, GQA_DH)
        s = jnp.einsum('bkgtd,bksd->bkgts', qg, keys).astype(f32) * GQA_DH ** -0.5
        p = jax.nn.softmax(s, axis=-1).astype(vals.dtype)
        o = jnp.einsum('bkgts,bksd->bkgtd', p, vals)
        return o.reshape(Bq, GQA_HEADS, T, GQA_DH)

    def finish(o):
        Bt, _, Lt, _ = o.shape
        return o.transpose(0, 2, 1, 3).reshape(Bt, Lt, GQA_HEADS * GQA_DH) @ w_out

    keys = jnp.concatenate([kc, k], axis=2)
    vals = jnp.concatenate([vc, v], axis=2)
    y = finish(sweep_query_blocks(lambda qb: attend(qb, keys, vals), q))
    yc = finish(attend(qc, kc, vc)) if ctx_out else None
    return y, yc


def gla_chunk_scan(q, k, v, g, s0):
    f32 = jnp.float32
    B, H, L, _ = q.shape
    n = L // GLA_CHUNK
    mask = jnp.tril(jnp.ones((GLA_CHUNK, GLA_CHUNK), dtype=bool))[:, :, None]

    def to_chunks(t):
        return t.reshape(B, H, n, GLA_CHUNK, t.shape[-1]).transpose(2, 0, 1, 3, 4)

    def step(S, inp):
        qc, kc, vc, gc = inp
        qf, kf, vf = qc.astype(f32), kc.astype(f32), vc.astype(f32)
        b = jnp.cumsum(gc.astype(f32), axis=-2)
        o_inter = jnp.einsum('bhtk,bhkv->bhtv', qf * jnp.exp(b), S)
        rel = jnp.where(mask, b[:, :, :, None, :] - b[:, :, None, :, :], -jnp.inf)
        A = jnp.einsum('bhtk,bhtsk,bhsk->bhts', qf, jnp.exp(rel), kf)
        o = o_inter + jnp.einsum('bhts,bhsv->bhtv', A, vf)
        b_last = b[:, :, -1, :]
        S = jnp.exp(b_last)[..., None] * S + jnp.einsum(
            'bhsk,bhsv->bhkv', kf * jnp.exp(b_last[:, :, None, :] - b), vf)
        return S, o.astype(v.dtype)

    S, o = lax.scan(step, s0, (to_chunks(q), to_chunks(k), to_chunks(v), to_chunks(g)))
    o = o.transpose(1, 2, 0, 3, 4).reshape(B, H, L, v.shape[-1])
    return o, S


def gla_mixer(h, hc, w_in, gw1_f, gw2_f, gb_f, gw1_b, gw2_b, gb_b, out_norm, w_out, ctx_out):
    B = h.shape[0]

    def project(t):
        Bt, Lt, _ = t.shape
        q, k, v, r = jnp.split(t @ w_in, [GLA_DK, 2 * GLA_DK, 2 * GLA_DK + GLA_DV], axis=-1)

        def heads(z, e):
            return z.reshape(Bt, Lt, GLA_HEADS, e).transpose(0, 2, 1, 3)

        def log_decay(w1, w2, b):
            return heads(jax.nn.log_sigmoid(((t @ w1) @ w2 + b).astype(jnp.float32)) / GLA_TAU, GLA_DK_H)

        return (heads(q, GLA_DK_H) * GLA_DK_H ** -0.5, heads(k, GLA_DK_H), heads(v, GLA_DV_H), r,
                log_decay(gw1_f, gw2_f, gb_f), log_decay(gw1_b, gw2_b, gb_b))

    def flip(z):
        return jnp.flip(z, axis=2)

    q, k, v, r, g_f, g_b = project(h)
    qc, kc, vc, rc, gc_f, gc_b = project(hc)
    zeros = jnp.zeros((B, GLA_HEADS, GLA_DK_H, GLA_DV_H), jnp.float32)
    oc_f, sc_f = gla_chunk_scan(qc, kc, vc, gc_f, zeros)
    oc_b, sc_b = gla_chunk_scan(flip(qc), flip(kc), flip(vc), flip(gc_b), zeros)
    o_f, _ = gla_chunk_scan(q, k, v, g_f, sc_f)
    o_b, _ = gla_chunk_scan(flip(q), flip(k), flip(v), flip(g_b), sc_b)

    def finish(o, gate_in):
        Bt, _, Lt, _ = o.shape
        o = rmsnorm(o, out_norm).transpose(0, 2, 1, 3).reshape(Bt, Lt, GLA_DV)
        return (o * jax.nn.silu(gate_in)) @ w_out

    y = finish(o_f + flip(o_b), r)
    yc = finish(oc_f + flip(oc_b), rc) if ctx_out else None
    return y, yc


def setup_inputs(seed: int = 0) -> dict:
    key = jax.random.key(seed)
    ks = jax.random.split(key, 32)
    f32 = jnp.float32
    D = D_MODEL

    def nrm(k, shape, scale):
        return scale * jax.random.normal(k, shape, f32)

    def gain(k, shape):
        return 1.0 + 0.02 * jax.random.normal(k, shape, f32)

    return {
        'x': nrm(ks[0], (BATCH, SEQ, D), 1.0),
        'c': nrm(ks[1], (BATCH, D), 1.0),
        'ctx': nrm(ks[2], (BATCH, CTX_LEN, D), 1.0),
        'c_ctx': nrm(ks[3], (D,), 1.0),
        'ada_w': nrm(ks[4], (DEPTH, D, 6 * D), 0.5 * D ** -0.5),
        'ada_b': nrm(ks[5], (DEPTH, 6 * D), 0.01),
        'norm1_g': gain(ks[6], (DEPTH, D)),
        'norm2_g': gain(ks[7], (DEPTH, D)),
        'ffn_w_gu': nrm(ks[8], (DEPTH, D, 2 * FFN_HIDDEN), D ** -0.5),
        'ffn_w_down': nrm(ks[9], (DEPTH, FFN_HIDDEN, D), FFN_HIDDEN ** -0.5),
        'diff_w_in': nrm(ks[10], (N_A, D, 3 * D), D ** -0.5),
        'diff_w_out': nrm(ks[11], (N_A, D, D), D ** -0.5),
        'diff_q_norm': gain(ks[12], (N_A, DIFF_DH)),
        'diff_k_norm': gain(ks[13], (N_A, DIFF_DH)),
        'diff_lambda_q1': nrm(ks[14], (N_A, DIFF_DH), 0.1),
        'diff_lambda_k1': nrm(ks[15], (N_A, DIFF_DH), 0.1),
        'diff_lambda_q2': nrm(ks[16], (N_A, DIFF_DH), 0.1),
        'diff_lambda_k2': nrm(ks[17], (N_A, DIFF_DH), 0.1),
        'diff_subln': gain(ks[18], (N_A, 2 * DIFF_DH)),
        'gqa_w_in': nrm(ks[19], (N_B, D, GQA_HEADS * GQA_DH + 2 * GQA_KV_HEADS * GQA_DH), D ** -0.5),
        'gqa_w_out': nrm(ks[20], (N_B, GQA_HEADS * GQA_DH, D), (GQA_HEADS * GQA_DH) ** -0.5),
        'gqa_q_norm': gain(ks[21], (N_B, GQA_DH)),
        'gqa_k_norm': gain(ks[22], (N_B, GQA_DH)),
        'gla_w_in': nrm(ks[23], (N_C, D, 2 * GLA_DK + 2 * GLA_DV), D ** -0.5),
        'gla_gate_w1_fwd': nrm(ks[24], (N_C, D, GLA_GATE_RANK), D ** -0.5),
        'gla_gate_w2_fwd': nrm(ks[25], (N_C, GLA_GATE_RANK, GLA_DK), GLA_GATE_RANK ** -0.5),
        'gla_gate_b_fwd': nrm(ks[26], (N_C, GLA_DK), 0.01),
        'gla_gate_w1_bwd': nrm(ks[27], (N_C, D, GLA_GATE_RANK), D ** -0.5),
        'gla_gate_w2_bwd': nrm(ks[28], (N_C, GLA_GATE_RANK, GLA_DK), GLA_GATE_RANK ** -0.5),
        'gla_gate_b_bwd': nrm(ks[29], (N_C, GLA_DK), 0.01),
        'gla_out_norm': gain(ks[30], (N_C, GLA_DV_H)),
        'gla_w_out': nrm(ks[31], (N_C, GLA_DV, D), GLA_DV ** -0.5),
    }


def reference(x, c, ctx, c_ctx, ada_w, ada_b, norm1_g, norm2_g, ffn_w_gu, ffn_w_down,
              diff_w_in, diff_w_out, diff_q_norm, diff_k_norm, diff_lambda_q1, diff_lambda_k1,
              diff_lambda_q2, diff_lambda_k2, diff_subln, gqa_w_in, gqa_w_out, gqa_q_norm, gqa_k_norm,
              gla_w_in, gla_gate_w1_fwd, gla_gate_w2_fwd, gla_gate_b_fwd, gla_gate_w1_bwd,
              gla_gate_w2_bwd, gla_gate_b_bwd, gla_out_norm, gla_w_out):
    xc = ctx
    cond_lat = jax.nn.silu(c)
    cond_ctx = jax.nn.silu(c_ctx)
    for i in range(DEPTH):
        last = i == DEPTH - 1
        mod = (cond_lat @ ada_w[i] + ada_b[i])[:, None, :]
        modc = (cond_ctx @ ada_w[i] + ada_b[i])[None, None, :]
        sh1, sc1, g1, sh2, sc2, g2 = jnp.split(mod, 6, axis=-1)
        csh1, csc1, cg1, csh2, csc2, cg2 = jnp.split(modc, 6, axis=-1)

        h = modulate(rmsnorm(x, norm1_g[i]), sh1, sc1)
        hc = modulate(rmsnorm(xc, norm1_g[i]), csh1, csc1)
        kind, j = i % N_MIXERS, i // N_MIXERS
        if kind == 0:
            lambda_init = 0.8 - 0.6 * math.exp(-0.3 * i)
            y, yc = diff_attention(h, hc, diff_w_in[j], diff_w_out[j], diff_q_norm[j], diff_k_norm[j],
                                   diff_lambda_q1[j], diff_lambda_k1[j], diff_lambda_q2[j],
                                   diff_lambda_k2[j], diff_subln[j], lambda_init, not last)
        elif kind == 1:
            y, yc = gqa_attention(h, hc, gqa_w_in[j], gqa_w_out[j], gqa_q_norm[j], gqa_k_norm[j], not last)
        else:
            y, yc = gla_mixer(h, hc, gla_w_in[j], gla_gate_w1_fwd[j], gla_gate_w2_fwd[j], gla_gate_b_fwd[j],
                              gla_gate_w1_bwd[j], gla_gate_w2_bwd[j], gla_gate_b_bwd[j],
                              gla_out_norm[j], gla_w_out[j], not last)

        x = x + g1 * y
        x = x + g2 * swiglu(modulate(rmsnorm(x, norm2_g[i]), sh2, sc2), ffn_w_gu[i], ffn_w_down[i])
        if not last:
            xc = xc + cg1 * yc
            xc = xc + cg2 * swiglu(modulate(rmsnorm(xc, norm2_g[i]), csh2, csc2), ffn_w_gu[i], ffn_w_down[i])
    return x
```

```python
import math
from contextlib import ExitStack
import numpy as np
import concourse.bass as bass
import concourse.mybir as mybir
from concourse.bass_utils import run_bass_kernel_spmd

F32 = mybir.dt.float32
BF16 = mybir.dt.bfloat16
AF = mybir.ActivationFunctionType
ALU = mybir.AluOpType

D = 1024
KC = 8
SEQ = 8192
NCTX = 256
T = SEQ + NCTX
NT = T // 128
DEPTH = 4
FH = 2816
FC = FH // 128
EPS = 1e-6
GRID_W = 64
DEBUG = False


class Ev:
    __slots__ = ("key", "cnt")

    def __init__(self, key, cnt):
        self.key, self.cnt = key, cnt


class Eng:
    def __init__(self, name, e, sem):
        self.name, self.e, self.sem = name, e, sem
        self.cnt = 0
        self.seen = {}
        self.pending = []


class DSem:
    def __init__(self, sem):
        self.sem = sem
        self.cnt = 0


class Buf:
    def __init__(self, name, dma=False):
        self.name, self.dsem, self.dma = name, None, dma
        self.w = {}
        self.r = {}


class TB:
    def __init__(self, t, b):
        self.t, self.b = t, b


class Ring:
    def __init__(self, items):
        self.items, self.i = items, 0

    def next(self):
        x = self.items[self.i % len(self.items)]
        self.i += 1
        return x


class K:
    def __init__(self, nc):
        self.nc = nc
        self.root = ExitStack()
        self.scopes = [self.root]
        self.uid = 0
        self.E = {}
        for name, e in (("pe", nc.tensor), ("act", nc.scalar), ("dve", nc.vector),
                        ("pool", nc.gpsimd), ("sp", nc.sync)):
            self.E[name] = Eng(name, e, self.root.enter_context(nc.semaphore("s_" + name)))
        self.bar_sem = self.root.enter_context(nc.semaphore("s_bar"))
        self.bar_cnt = 0
        self.dsem_free = {"sp": [], "pool": []}
        self.dsem_all = []
        self.scope_dsems = [[]]
        self.ps = self.root.enter_context(nc.psum_tensor("ps", [128, 8, 512], F32))
        self.pb = [Buf("pb%d" % i) for i in range(8)]

    def name(self, n):
        self.uid += 1
        return "%s_%d" % (n, self.uid)

    def tile(self, name, shape, dt):
        return self.scopes[-1].enter_context(self.nc.sbuf_tensor(self.name(name), shape, dt))

    def get_dsem(self, q):
        if self.dsem_free[q]:
            d = self.dsem_free[q].pop()
        else:
            d = DSem(self.root.enter_context(self.nc.semaphore(self.name("d" + q))))
            d.q = q
            self.dsem_all.append(d)
        self.scope_dsems[-1].append(d)
        return d

    def tb(self, name, shape, dt, dma=False):
        t = self.tile(name, shape, dt)
        return TB(t, Buf(name, dma))

    def ring(self, name, n, shape, dt, dma=False):
        return Ring([self.tb("%s%d" % (name, i), shape, dt, dma) for i in range(n)])

    def phase_begin(self):
        st = ExitStack()
        self.scopes.append(st)
        self.scope_dsems.append([])

    def phase_end(self):
        self.barrier()
        st = self.scopes.pop()
        st.close()
        for d in self.scope_dsems.pop():
            self.dsem_free[d.q].append(d)

    def barrier(self):
        sp = self.E["sp"]
        for E in self.E.values():
            assert not E.pending, "pending events at barrier on " + E.name
            if E is not sp and E.cnt > sp.seen.get(E, 0):
                sp.e.wait_ge(E.sem, E.cnt)
        for d in self.dsem_all:
            if d.cnt > sp.seen.get(d, 0):
                sp.e.wait_ge(d.sem, d.cnt)
        self.bar_cnt += 1
        sp.e.sem_inc(self.bar_sem, 1)
        for E in self.E.values():
            if E is not sp:
                E.e.wait_ge(self.bar_sem, self.bar_cnt)
            for Y in self.E.values():
                E.seen[Y] = Y.cnt
            for d in self.dsem_all:
                E.seen[d] = d.cnt

    def _deps(self, reads, writes, skip_key=None):
        deps = []
        for b in reads:
            deps.extend(b.w.values())
        for b in writes:
            for ev in b.w.values():
                if ev.key is not skip_key:
                    deps.append(ev)
            deps.extend(b.r.values())
        return deps

    def _wait(self, E, deps):
        pe = self.E["pe"]
        for ev in deps:
            if ev.key is E and E is pe:
                continue
            assert ev.cnt is not None, "wait on unsignaled event"
            if E.seen.get(ev.key, 0) >= ev.cnt:
                continue
            E.e.wait_ge(ev.key.sem, ev.cnt)
            E.seen[ev.key] = ev.cnt

    def _record(self, ev, reads, writes):
        for b in reads:
            b.r[ev.key] = ev
        for b in writes:
            b.w = {ev.key: ev}
            b.r = {}

    def op(self, eng, fn, reads=(), writes=(), signal=True):
        E = self.E[eng]
        self._wait(E, self._deps(reads, writes))
        inst = fn(E.e)
        if signal:
            E.cnt += 1
            inst.then_inc(E.sem, 1)
            ev = Ev(E, E.cnt)
            for p in E.pending:
                p.cnt = E.cnt
            E.pending = []
        else:
            ev = Ev(E, None)
            E.pending.append(ev)
        self._record(ev, reads, writes)
        return ev

    def mm_group(self, out, pairs, reads, writes, signal=True, first=True, last=True, skip=False):
        E = self.E["pe"]
        self._wait(E, self._deps(reads, writes))
        n = len(pairs)
        inst = None
        for i, (l, r) in enumerate(pairs):
            if skip:
                inst = E.e.matmul(out, l, r, start=(first and i == 0), stop=(last and i == n - 1),
                                  skip_group_check=True)
            else:
                inst = E.e.matmul(out, l, r, start=(first and i == 0), stop=(last and i == n - 1))
        if signal:
            E.cnt += 1
            inst.then_inc(E.sem, 1)
            ev = Ev(E, E.cnt)
            for p in E.pending:
                p.cnt = E.cnt
            E.pending = []
        else:
            ev = Ev(E, None)
            E.pending.append(ev)
        self._record(ev, reads, writes)
        return ev

    def dma(self, q, out, in_, reads=(), writes=(), dsem=None, **kw):
        Q = self.E[q]
        if dsem is None:
            for b in list(writes) + list(reads):
                if b.dma:
                    if b.dsem is None:
                        b.dsem = self.get_dsem(q)
                    dsem = b.dsem
                    break
        assert dsem is not None and dsem.q == q
        self._wait(Q, self._deps(reads, writes, skip_key=dsem))
        inst = Q.e.dma_start(out=out, in_=in_, **kw)
        dsem.cnt += 16
        inst.then_inc(dsem.sem, 16)
        ev = Ev(dsem, dsem.cnt)
        for b in reads:
            b.r[ev.key] = ev
        for b in writes:
            b.w = {ev.key: ev}
            b.r = {}
        return ev

    def wait_all_dma(self, eng, bufs):
        E = self.E[eng]
        for b in bufs:
            self._wait(E, list(b.r.values()) + list(b.w.values()))


def run_pipeline(jobs):
    n = len(jobs)
    if n == 0:
        return
    S = max(len(j) for j in jobs)
    for i in range(n + S - 1):
        for s in range(S):
            ji = i - s
            if 0 <= ji < n and s < len(jobs[ji]) and jobs[ji][s] is not None:
                jobs[ji][s]()


class Prog:
    def __init__(self, layers, first, last_launch):
        self.layers = layers
        self.first = first
        self.final = last_launch
        nc = bass.Bass("TRN2", target_bir_lowering=False)
        self.nc = nc
        self.k = K(nc)
        self.inputs = {}
        self.build()

    def din(self, name, shape):
        t = self.nc.dram_tensor(name, list(shape), F32, kind="ExternalInput")
        self.inputs[name] = tuple(shape)
        return t.ap()

    def build(self):
        nc, k = self.nc, self.k
        Ls = self.layers
        if self.first:
            self.x_in = self.din("x", (SEQ, D))
            self.ctx_in = self.din("ctx", (NCTX, D))
            self.XT = nc.dram_tensor("XT", [D, T], F32).ap()
        else:
            self.XT_in = self.din("xt_in", (D, T))
            self.XT = self.XT_in
        if self.final:
            self.OUT = nc.dram_tensor("out", [SEQ, D], F32, kind="ExternalOutput").ap()
            self.XTw = self.XT if self.first else nc.dram_tensor("XTs", [D, T], F32).ap()
        else:
            self.XTw = nc.dram_tensor("xt_out", [D, T], F32, kind="ExternalOutput").ap()
        self.c_in = self.din("c", (D,))
        self.cctx_in = self.din("c_ctx", (D,))
        self.W = {}
        need = _needed(Ls)
        for nm, shp in WEIGHT_SHAPES.items():
            if nm in need:
                self.W[nm] = self.din(nm, shp)
        for nm, shp in CONST_SHAPES.items():
            if nm in need:
                self.W[nm] = self.din(nm, shp)
        dk = "ExternalOutput" if DEBUG else "Internal"
        self.QT = nc.dram_tensor("QT", [D, T], BF16, kind=dk).ap()
        self.KTd = nc.dram_tensor("KTd", [D, T], BF16, kind=dk).ap()
        self.VH = nc.dram_tensor("VH", [8, T, 128], BF16, kind=dk).ap()
        self.OT = nc.dram_tensor("OT", [D, T], BF16, kind=dk).ap()

        self.ident_f = k.tb("ident_f", [128, 128], F32, dma=True)
        self.ones_f = k.tb("ones_f", [128, 128], F32)
        self.ones_b = k.tb("ones_b", [128, 128], BF16)
        self.bd64 = k.tb("bd64", [128, 128], BF16, dma=True)
        self.r64t = k.tb("r64t", [128, 128], BF16, dma=True)
        self.r128t = k.tb("r128t", [128, 128], BF16, dma=True)
        self.mod = k.tb("mod", [128, DEPTH, 48, 2], F32)
        self.g1t = k.tb("g1t", [128, DEPTH, 8, 2], F32)
        self.g2t = k.tb("g2t", [128, DEPTH, 8, 2], F32)
        k.dma("sp", self.ident_f.t[:], self.W["c_ident"], writes=[self.ident_f.b])
        k.dma("pool", self.bd64.t[:], self.W["c_bd64"], writes=[self.bd64.b])
        k.dma("pool", self.r64t.t[:], self.W["c_r64t"], writes=[self.r64t.b])
        k.dma("pool", self.r128t.t[:], self.W["c_r128t"], writes=[self.r128t.b])
        k.op("dve", lambda e: e.memset(self.ones_f.t[:], 1.0), writes=[self.ones_f.b])
        k.op("dve", lambda e: e.memset(self.ones_b.t[:], 1.0), writes=[self.ones_b.b])

        self.prologue_mod()
        if self.first:
            self.prologue_x()
        for li, L in enumerate(Ls):
            self.xt_r = self.XT if li == 0 else self.XTw
            last = (L == DEPTH - 1)
            kind = L % 3
            if kind in (0, 1):
                self.p1_attn(L, kind)
                self.p2_attn(L, kind, last)
            else:
                self.p1_gla(L)
                self.p2_gla(L, last)
            self.p3(L, kind, last)
        k.barrier()
        k.root.close()

    def prologue_mod(self):
        k, W = self.k, self.W
        k.phase_begin()
        craw = k.tb("craw", [128, 8, 2], F32, dma=True)
        cnd = k.tb("cnd", [128, 8, 2], F32)
        adb = k.tb("adb", [128, DEPTH, 48], F32, dma=True)
        n1g = k.tb("n1g", [128, DEPTH, 8], F32, dma=True)
        n2g = k.tb("n2g", [128, DEPTH, 8], F32, dma=True)
        aw = k.ring("aw", 2, [128, 6144], F32, dma=True)
        tmp = k.tb("modtmp", [128, 8, 2], F32)
        k.dma("sp", craw.t[:, :, 0], self.c_in.rearrange("(c p) -> p c", p=128), writes=[craw.b],
              allow_slow_non_contiguous=True)
        k.dma("sp", craw.t[:, :, 1], self.cctx_in.rearrange("(c p) -> p c", p=128), writes=[craw.b],
              allow_slow_non_contiguous=True)
        k.dma("sp", adb.t[:], W["ada_b"].rearrange("l (c p) -> p l c", p=128), writes=[adb.b],
              allow_slow_non_contiguous=True)
        k.dma("sp", n1g.t[:], W["norm1_g"].rearrange("l (c p) -> p l c", p=128), writes=[n1g.b],
              allow_slow_non_contiguous=True)
        k.dma("sp", n2g.t[:], W["norm2_g"].rearrange("l (c p) -> p l c", p=128), writes=[n2g.b],
              allow_slow_non_contiguous=True)
        k.op("act", lambda e: e.activation(out=cnd.t[:], in_=craw.t[:], func=AF.Silu),
             reads=[craw.b], writes=[cnd.b])
        ps = k.ps
        for L in self.layers:
            for kc in range(KC):
                a = aw.next()
                k.dma("sp", a.t[:], W["ada_w"][L, kc * 128:(kc + 1) * 128, :], writes=[a.b])
                for f in range(48):
                    k.mm_group(ps[:, 0, 2 * f:2 * f + 2], [(a.t[:, f * 128:(f + 1) * 128], cnd.t[:, kc, :])],
                               reads=[a.b, cnd.b], writes=[k.pb[0]], signal=(f == 47),
                               first=(kc == 0 and f == 0), last=(kc == KC - 1), skip=True)
            for v in range(2):
                k.op("dve", lambda e, v=v, L=L: e.tensor_tensor(
                    out=self.mod.t[:, L, :, v], in0=ps[:, 0, 0:96].rearrange("p (f v) -> p f v", v=2)[:, :, v],
                    in1=adb.t[:, L, :], op=ALU.add),
                    reads=[k.pb[0], adb.b], writes=[self.mod.b])
            for (gt, ng, off) in ((self.g1t, n1g, 8), (self.g2t, n2g, 32)):
                for v in range(2):
                    k.op("dve", lambda e, v=v, L=L, off=off: e.tensor_scalar(
                        out=tmp.t[:, :, v], in0=self.mod.t[:, L, off:off + 8, v], scalar1=1.0, scalar2=None,
                        op0=ALU.add), reads=[self.mod.b], writes=[tmp.b])
                    k.op("dve", lambda e, v=v, L=L, gt=gt, ng=ng: e.tensor_tensor(
                        out=gt.t[:, L, :, v], in0=tmp.t[:, :, v], in1=ng.t[:, L, :], op=ALU.mult),
                        reads=[tmp.b, ng.b], writes=[gt.b])
        k.phase_end()

    def prologue_x(self):
        k = self.k
        k.phase_begin()
        xin = k.ring("xin", 8, [128, D], F32, dma=True)
        xst = k.ring("xst", 2, [128, KC, 512], F32, dma=True)
        ps = k.ps
        bank = [0]
        blocks = [(0, NCTX)] + [(NCTX + 512 * i, 512) for i in range(SEQ // 512)]
        cnt = [0]
        for (t0, n) in blocks:
            nt = n // 128
            tiles = []
            for i in range(nt):
                xi = xin.next()
                src = self.ctx_in[i * 128:(i + 1) * 128, :] if t0 == 0 else \
                    self.x_in[t0 - NCTX + i * 128: t0 - NCTX + (i + 1) * 128, :]
                k.dma("sp", xi.t[:], src, writes=[xi.b])
                tiles.append(xi)
            st = xst.next()
            for c in range(KC):
                b = bank[0] % 8
                bank[0] += 1
                E = k.E["pe"]
                k._wait(E, k._deps([t.b for t in tiles] + [self.ident_f.b], [k.pb[b]]))
                inst = None
                for i, xi in enumerate(tiles):
                    inst = E.e.transpose(ps[:, b, i * 128:(i + 1) * 128], xi.t[:, c * 128:(c + 1) * 128],
                                         self.ident_f.t[:])
                E.cnt += 1
                inst.then_inc(E.sem, 1)
                ev = Ev(E, E.cnt)
                k._record(ev, [t.b for t in tiles], [k.pb[b]])
                eng = "act" if cnt[0] % 2 == 0 else "dve"
                cnt[0] += 1
                if eng == "act":
                    k.op("act", lambda e, b=b, c=c, st=st, n=n: e.activation(
                        out=st.t[:, c, 0:n], in_=ps[:, b, 0:n], func=AF.Copy), reads=[k.pb[b]], writes=[st.b])
                else:
                    k.op("dve", lambda e, b=b, c=c, st=st, n=n: e.tensor_copy(
                        out=st.t[:, c, 0:n], in_=ps[:, b, 0:n]), reads=[k.pb[b]], writes=[st.b])
            k.dma("sp", self.XT.rearrange("(c p) t -> p c t", p=128)[:, :, t0:t0 + n], st.t[:, :, 0:n],
                  reads=[st.b])
        k.phase_end()

    def norm_part1(self, env, blk, src_ap, xt, sq=None):
        k = self.k
        t0, n, v = blk
        if src_ap is not None:
            k.dma("sp", xt.t[:, :, 0:n], src_ap.rearrange("(c p) t -> p c t", p=128)[:, :, t0:t0 + n],
                  writes=[xt.b])
        if sq is None:
            sq = env["nsq"].next()
        for c in range(KC):
            k.op("act", lambda e, c=c: e.activation(out=sq.t[:, c, 0:n], in_=xt.t[:, c, 0:n], func=AF.Square),
                 reads=[xt.b], writes=[sq.b])
        return sq

    def norm_part2(self, env, blk, xt, sq, L, gt, sh_off, hT, bank):
        k = self.k
        t0, n, v = blk
        ps = k.ps
        k.mm_group(ps[:, bank, 0:n], [(self.ones_b.t[:], sq.t[:, c, 0:n]) for c in range(KC)],
                   reads=[sq.b, self.ones_b.b], writes=[k.pb[bank]])
        ln = env["nln"].next()
        rs = env["nrs"].next()
        k.op("act", lambda e: e.activation(out=ln.t[:, 0:n], in_=ps[:, bank, 0:n], func=AF.Ln,
                                           scale=1.0 / D, bias=EPS), reads=[k.pb[bank]], writes=[ln.b])
        k.op("act", lambda e: e.activation(out=rs.t[:, 0:n], in_=ln.t[:, 0:n], func=AF.Exp, scale=-0.5),
             reads=[ln.b], writes=[rs.b])
        for c in range(KC):
            tm = env["ntm"].next()
            k.op("dve", lambda e, c=c, tm=tm: e.tensor_tensor(out=tm.t[:, 0:n], in0=xt.t[:, c, 0:n],
                                                             in1=rs.t[:, 0:n], op=ALU.mult),
                 reads=[xt.b, rs.b], writes=[tm.b])
            k.op("act", lambda e, c=c, tm=tm: e.activation(
                out=hT.t[:, c, 0:n], in_=tm.t[:, 0:n], func=AF.Identity,
                scale=gt.t[:, L, c, v:v + 1], bias=self.mod.t[:, L, sh_off + c, v:v + 1]),
                reads=[tm.b, gt.b, self.mod.b], writes=[hT.b])

    def norm_env(self, n=512):
        k = self.k
        if n == 512:
            return {
                "nsq": k.ring("nsq", 2, [128, KC, n], BF16),
                "nln": k.ring("nln", 2, [128, n], F32),
                "nrs": k.ring("nrs", 2, [128, n], F32),
                "ntm": k.ring("ntm", 3, [128, n], F32),
            }
        return {
            "nln": k.ring("nln", 1, [128, n], F32),
            "nrs": k.ring("nrs", 1, [128, n], F32),
            "ntm": k.ring("ntm", 2, [128, n], F32),
        }

    def p1_attn(self, L, kind):
        k, W = self.k, self.W
        j = L // 3
        ps = k.ps
        if kind == 0:
            w_in = W["diff_w_in"][j]
            ncols, nq, nk, vcols, dh = 3072, 8, 8, 1024, 64
            qn_src, kn_src = W["diff_q_norm"][j], W["diff_k_norm"][j]
            bd, rt = self.bd64, self.r64t
            cos_d, sin_d = W["c_cos64"], W["c_sin64"]
        else:
            w_in = W["gqa_w_in"][j]
            ncols, nq, nk, vcols, dh = 1536, 8, 2, 256, 128
            qn_src, kn_src = W["gqa_q_norm"][j], W["gqa_k_norm"][j]
            bd, rt = self.ones_b, self.r128t
            cos_d, sin_d = W["c_cos128"], W["c_sin128"]
        k.phase_begin()
        win = k.tb("win", [128, KC, ncols], BF16, dma=True)
        for c in range(KC):
            k.dma("pool", win.t[:, c, :], w_in[c * 128:(c + 1) * 128, :], writes=[win.b])
        ncol = k.tb("ncol", [128, 2], F32, dma=True)
        for i, src in enumerate((qn_src, kn_src)):
            for r in range(128 // dh):
                k.dma("sp", ncol.t[r * dh:(r + 1) * dh, i:i + 1], src.rearrange("(p o) -> p o", o=1),
                      writes=[ncol.b], allow_slow_non_contiguous=True)
        env = self.norm_env(512)
        xtr = k.ring("xt", 2, [128, KC, 512], F32, dma=True)
        hTr = k.ring("hT", 2, [128, KC, 512], BF16)
        cosr = k.ring("cos", 2, [128, 512], F32, dma=True)
        sinr = k.ring("sin", 2, [128, 512], F32, dma=True)
        sqq = k.ring("sqq", 3, [128, 512], BF16)
        lnq = k.ring("lnq", 2, [128, 512], F32)
        rsq = k.ring("rsq", 2, [128, 512], F32)
        qnr = k.ring("qn", 3, [128, 512], BF16)
        t1r = k.ring("t1", 2, [128, 512], F32)
        t2r = k.ring("t2", 2, [128, 512], F32)
        qor = k.ring("qo", 3, [128, 512], BF16, dma=True)
        vst = k.ring("vst", 3, [128, 512], BF16, dma=True)
        pj = Ring([1, 2, 3])
        pm = Ring([4, 5])
        pr = Ring([6, 7])
        blocks = [(0, NCTX, 1)] + [(NCTX + 512 * i, 512, 0) for i in range(SEQ // 512)]
        state = {}
        jobs = []
        cp = [0]

        def mk_block_jobs(bi):
            blk = blocks[bi]
            t0, n, v = blk

            def part1():
                xt = xtr.next()
                sq = self.norm_part1(env, blk, self.xt_r, xt)
                cs, sn = cosr.next(), sinr.next()
                k.dma("sp", cs.t[:, 0:n], cos_d[:, t0:t0 + n], writes=[cs.b])
                k.dma("sp", sn.t[:, 0:n], sin_d[:, t0:t0 + n], writes=[sn.b])
                state[bi] = {"xt": xt, "sq": sq, "cos": cs, "sin": sn}

            def part2():
                st = state[bi]
                hT = hTr.next()
                st["hT"] = hT
                self.norm_part2(env, blk, st["xt"], st["sq"], L, self.g1t, 0, hT, 0)

            out = [[part1], [part2]]
            for ci in range(nq + nk):
                isq = ci < nq
                col0 = ci * 128
                d = {}

                def A(d=d, col0=col0):
                    st = state[bi]
                    d["b"] = pj.next()
                    k.mm_group(ps[:, d["b"], 0:n],
                               [(win.t[:, c, col0:col0 + 128], st["hT"].t[:, c, 0:n]) for c in range(KC)],
                               reads=[win.b, st["hT"].b], writes=[k.pb[d["b"]]])

                def B(d=d):
                    d["sq"] = sqq.next()
                    k.op("act", lambda e: e.activation(out=d["sq"].t[:, 0:n], in_=ps[:, d["b"], 0:n],
                                                       func=AF.Square), reads=[k.pb[d["b"]]], writes=[d["sq"].b])
                    d["m"] = pm.next()
                    k.mm_group(ps[:, d["m"], 0:n], [(bd.t[:], d["sq"].t[:, 0:n])],
                               reads=[bd.b, d["sq"].b], writes=[k.pb[d["m"]]])

                def C(d=d, isq=isq):
                    ln, rs, qn = lnq.next(), rsq.next(), qnr.next()
                    d["qn"] = qn
                    k.op("act", lambda e: e.activation(out=ln.t[:, 0:n], in_=ps[:, d["m"], 0:n], func=AF.Ln,
                                                       scale=1.0 / dh, bias=EPS), reads=[k.pb[d["m"]]], writes=[ln.b])
                    k.op("act", lambda e: e.activation(out=rs.t[:, 0:n], in_=ln.t[:, 0:n], func=AF.Exp, scale=-0.5),
                         reads=[ln.b], writes=[rs.b])
                    ci_ = 0 if isq else 1
                    k.op("dve", lambda e: e.scalar_tensor_tensor(
                        out=qn.t[:, 0:n], in0=ps[:, d["b"], 0:n], scalar=ncol.t[:, ci_:ci_ + 1], in1=rs.t[:, 0:n],
                        op0=ALU.mult, op1=ALU.mult), reads=[k.pb[d["b"]], ncol.b, rs.b], writes=[qn.b])
                    d["r"] = pr.next()
                    k.mm_group(ps[:, d["r"], 0:n], [(rt.t[:], qn.t[:, 0:n])], reads=[rt.b, qn.b],
                               writes=[k.pb[d["r"]]])

                def Dd(d=d, isq=isq, ci=ci):
                    st = state[bi]
                    t1, t2, qo = t1r.next(), t2r.next(), qor.next()
                    k.op("dve", lambda e: e.tensor_tensor(out=t1.t[:, 0:n], in0=d["qn"].t[:, 0:n],
                                                          in1=st["cos"].t[:, 0:n], op=ALU.mult),
                         reads=[d["qn"].b, st["cos"].b], writes=[t1.b])
                    k.op("dve", lambda e: e.tensor_tensor(out=t2.t[:, 0:n], in0=ps[:, d["r"], 0:n],
                                                          in1=st["sin"].t[:, 0:n], op=ALU.mult),
                         reads=[k.pb[d["r"]], st["sin"].b], writes=[t2.b])
                    k.op("pool", lambda e: e.tensor_tensor(out=qo.t[:, 0:n], in0=t1.t[:, 0:n], in1=t2.t[:, 0:n],
                                                           op=ALU.add), reads=[t1.b, t2.b], writes=[qo.b])
                    if isq:
                        dst = self.QT[ci * 128:(ci + 1) * 128, t0:t0 + n]
                    else:
                        dst = self.KTd[(ci - nq) * 128:(ci - nq + 1) * 128, t0:t0 + n]
                    k.dma("sp", dst, qo.t[:, 0:n], reads=[qo.b])

                out.append([A, B, C, Dd])
            for ti in range(n // 128):
                for cg in range(vcols // min(vcols, 512)):
                    cw = min(vcols, 512)
                    d = {}

                    def A(d=d, ti=ti, cg=cg, cw=cw):
                        st = state[bi]
                        d["b"] = pj.next()
                        c0 = nq * 128 + nk * 128 + cg * cw
                        k.mm_group(ps[:, d["b"], 0:cw],
                                   [(st["hT"].t[:, c, ti * 128:(ti + 1) * 128], win.t[:, c, c0:c0 + cw])
                                    for c in range(KC)],
                                   reads=[win.b, st["hT"].b], writes=[k.pb[d["b"]]])

                    def B(d=d, ti=ti, cg=cg, cw=cw):
                        vs = vst.next()
                        cp[0] += 1
                        if cp[0] % 2:
                            k.op("act", lambda e: e.activation(out=vs.t[:, 0:cw], in_=ps[:, d["b"], 0:cw],
                                                               func=AF.Copy), reads=[k.pb[d["b"]]], writes=[vs.b])
                        else:
                            k.op("dve", lambda e: e.tensor_copy(out=vs.t[:, 0:cw], in_=ps[:, d["b"], 0:cw]),
                                 reads=[k.pb[d["b"]]], writes=[vs.b])
                        h0 = cg * (cw // 128)
                        tt = t0 + ti * 128
                        dst = self.VH[h0:h0 + cw // 128, tt:tt + 128, :].rearrange("h t d -> t h d")
                        k.dma("sp", dst, vs.t[:, 0:cw].rearrange("p (h d) -> p h d", d=128), reads=[vs.b])

                    out.append([A, B])
            return out

        allj = []
        bj = [mk_block_jobs(bi) for bi in range(len(blocks))]
        allj.extend(bj[0][0:2])
        for bi in range(len(blocks)):
            body = bj[bi][2:]
            nxt = bj[bi + 1][0:2] if bi + 1 < len(blocks) else []
            if nxt:
                body = body[0:2] + [nxt[0]] + body[2:8] + [nxt[1]] + body[8:]
            allj.extend(body)
        run_pipeline(allj)
        k.phase_end()

    def p2_attn(self, L, kind, last):
        k, W = self.k, self.W
        j = L // 3
        ps = k.ps
        k.phase_begin()
        KTr = k.ring("KT", 2, [128, T], BF16, dma=True)
        VTr = k.ring("VT", 2, [128, NT, 128], BF16, dma=True)
        QTr = k.ring("QTt", 2, [128, T], BF16, dma=True)
        qblocks = ([] if last else [(0, NCTX, NCTX // 128)]) + [(NCTX + 512 * i, 512, NT) for i in range(SEQ // 512)]
        if kind == 0:
            dh = 64
            scale = dh ** -0.5
            PTr = k.ring("PT", 6, [128, 1024], BF16)
            accr = k.ring("acc", 2, [128, 1024], F32)
            tpr = k.ring("tp", 2, [128, 1024], BF16)
            r1r = k.ring("r1", 2, [128, 512], F32)
            r2r = k.ring("r2", 2, [128, 512], F32)
            a1r = k.ring("a1", 2, [128, 512], F32)
            a2r = k.ring("a2", 2, [128, 512], F32)
            ost = k.ring("ost", 3, [128, 512], BF16, dma=True)
            lambda_init = 0.8 - 0.6 * math.exp(-0.3 * L)
            lv = k.tb("lv", [64, 4], F32, dma=True)
            for i, nm in enumerate(("diff_lambda_q1", "diff_lambda_k1", "diff_lambda_q2", "diff_lambda_k2")):
                k.dma("sp", lv.t[:, i:i + 1], W[nm][j].rearrange("(p o) -> p o", o=1), writes=[lv.b],
                      allow_slow_non_contiguous=True)
            lp = k.tb("lp", [64, 2], F32)
            for i in range(2):
                k.op("dve", lambda e, i=i: e.tensor_tensor(out=lp.t[:, i:i + 1], in0=lv.t[:, 2 * i:2 * i + 1],
                                                           in1=lv.t[:, 2 * i + 1:2 * i + 2], op=ALU.mult),
                     reads=[lv.b], writes=[lp.b])
            k.mm_group(ps[:, 0, 0:2], [(self.ones_f.t[0:64, :], lp.t[:, 0:2])], reads=[self.ones_f.b, lp.b],
                       writes=[k.pb[0]])
            le = k.tb("le", [128, 2], F32)
            neglam = k.tb("neglam", [128, 1], F32)
            k.op("act", lambda e: e.activation(out=le.t[:], in_=ps[:, 0, 0:2], func=AF.Exp), reads=[k.pb[0]],
                 writes=[le.b])
            k.op("dve", lambda e: e.tensor_tensor(out=neglam.t[:], in0=le.t[:, 1:2], in1=le.t[:, 0:1],
                                                  op=ALU.subtract), reads=[le.b], writes=[neglam.b])
            k.op("dve", lambda e: e.tensor_scalar(out=neglam.t[:], in0=neglam.t[:], scalar1=-lambda_init,
                                                  scalar2=None, op0=ALU.add), reads=[neglam.b], writes=[neglam.b])
            sset = Ring([(0, 1), (2, 3)])
            heads = list(range(8))
            hd = {}

            def load_head(h):
                kt, vt, qt = KTr.next(), VTr.next(), QTr.next()
                k.dma("sp", kt.t[:], self.KTd[h * 128:(h + 1) * 128, :], writes=[kt.b])
                k.dma("sp", qt.t[:], self.QT[h * 128:(h + 1) * 128, :], writes=[qt.b])
                vsrc = self.VH[h].rearrange("(n p) d -> p n d", p=128)
                for a in range(0, NT, 22):
                    k.dma("sp", vt.t[:, a:min(a + 22, NT), :], vsrc[:, a:min(a + 22, NT), :], writes=[vt.b])
                hd[h] = (kt, vt, qt)

            units = []
            for h in heads:
                for (q0, n, nkt) in qblocks:
                    for kt in range(nkt):
                        units.append((h, q0, n, nkt, kt))

            def S(u, d):
                h, q0, n, nkt, kti = u
                kt, vt, qt = hd[h]
                s = sset.next()
                d["s"] = s
                E = k.E["pe"]
                k._wait(E, k._deps([kt.b, qt.b], [k.pb[s[0]], k.pb[s[1]]]))
                E.e.matmul(ps[:, s[0], 0:n], kt.t[0:64, kti * 128:(kti + 1) * 128], qt.t[0:64, q0:q0 + n],
                           start=True, stop=True)
                inst = E.e.matmul(ps[:, s[1], 0:n], kt.t[64:128, kti * 128:(kti + 1) * 128],
                                  qt.t[64:128, q0:q0 + n], start=True, stop=True)
                E.cnt += 1
                inst.then_inc(E.sem, 1)
                k._record(Ev(E, E.cnt), [kt.b, qt.b], [k.pb[s[0]], k.pb[s[1]]])

            def X(u, d):
                h, q0, n, nkt, kti = u
                s = d["s"]
                pt = PTr.next()
                d["pt"] = pt
                k.op("act", lambda e: e.activation(
                    out=pt.t[:, 0:2 * n].rearrange("p (a n) -> p a n", a=2), in_=ps[:, s[0]:s[0] + 2, 0:n],
                    func=AF.Exp, scale=scale), reads=[k.pb[s[0]], k.pb[s[1]]], writes=[pt.b])

            qst = {}

            def PV(u, d):
                h, q0, n, nkt, kti = u
                kt, vt, qt = hd[h]
                pt = d["pt"]
                E = k.E["pe"]
                k._wait(E, k._deps([vt.b, pt.b], [k.pb[4], k.pb[5]]))
                st, sp_ = (kti == 0), (kti == nkt - 1)
                E.e.matmul(ps[:, 4, 0:n], vt.t[:, kti, :], pt.t[:, 0:n], start=st, stop=sp_)
                inst = E.e.matmul(ps[:, 5, 0:n], vt.t[:, kti, :], pt.t[:, n:2 * n], start=st, stop=sp_)
                E.cnt += 1
                inst.then_inc(E.sem, 1)
                k._record(Ev(E, E.cnt), [vt.b, pt.b], [k.pb[4], k.pb[5]])
                if st:
                    qst["acc"] = accr.next()
                acc = qst["acc"]
                if kti % 2 == 0:
                    qst["prev"] = pt
                else:
                    pa = qst["prev"]
                    if kti == 1:
                        k.op("dve", lambda e: e.tensor_tensor(out=acc.t[:, 0:2 * n], in0=pa.t[:, 0:2 * n],
                                                              in1=pt.t[:, 0:2 * n], op=ALU.add),
                             reads=[pa.b, pt.b], writes=[acc.b])
                    else:
                        tp = tpr.next()
                        k.op("dve", lambda e: e.tensor_tensor(out=tp.t[:, 0:2 * n], in0=pa.t[:, 0:2 * n],
                                                              in1=pt.t[:, 0:2 * n], op=ALU.add),
                             reads=[pa.b, pt.b], writes=[tp.b])
                        k.op("dve", lambda e: e.tensor_tensor(out=acc.t[:, 0:2 * n], in0=acc.t[:, 0:2 * n],
                                                              in1=tp.t[:, 0:2 * n], op=ALU.add),
                             reads=[acc.b, tp.b], writes=[acc.b])
                if not sp_:
                    return None
                a1, a2 = a1r.next(), a2r.next()
                k.op("dve", lambda e: e.tensor_copy(out=a1.t[:, 0:n], in_=ps[:, 4, 0:n]), reads=[k.pb[4]],
                     writes=[a1.b])
                k.op("dve", lambda e: e.tensor_copy(out=a2.t[:, 0:n], in_=ps[:, 5, 0:n]), reads=[k.pb[5]],
                     writes=[a2.b])

                def partB():
                    k.mm_group(ps[:, 6, 0:n], [(self.ones_f.t[:], acc.t[:, 0:n])], reads=[self.ones_f.b, acc.b],
                               writes=[k.pb[6]])
                    k.mm_group(ps[:, 7, 0:n], [(self.ones_f.t[:], acc.t[:, n:2 * n])], reads=[self.ones_f.b, acc.b],
                               writes=[k.pb[7]])
                    r1, r2, o = r1r.next(), r2r.next(), ost.next()
                    k.op("dve", lambda e: e.reciprocal(out=r1.t[:, 0:n], in_=ps[:, 6, 0:n]), reads=[k.pb[6]],
                         writes=[r1.b])
                    k.op("dve", lambda e: e.reciprocal(out=r2.t[:, 0:n], in_=ps[:, 7, 0:n]), reads=[k.pb[7]],
                         writes=[r2.b])
                    k.op("dve", lambda e: e.tensor_tensor(out=a1.t[:, 0:n], in0=a1.t[:, 0:n], in1=r1.t[:, 0:n],
                                                          op=ALU.mult), reads=[a1.b, r1.b], writes=[a1.b])
                    k.op("dve", lambda e: e.tensor_tensor(out=a2.t[:, 0:n], in0=a2.t[:, 0:n], in1=r2.t[:, 0:n],
                                                          op=ALU.mult), reads=[a2.b, r2.b], writes=[a2.b])
                    k.op("dve", lambda e: e.scalar_tensor_tensor(
                        out=o.t[:, 0:n], in0=a2.t[:, 0:n], scalar=neglam.t[:, 0:1], in1=a1.t[:, 0:n],
                        op0=ALU.mult, op1=ALU.add), reads=[a2.b, neglam.b, a1.b], writes=[o.b])
                    k.dma("sp", self.OT[h * 128:(h + 1) * 128, q0:q0 + n], o.t[:, 0:n], reads=[o.b])
                return partB
        else:
            dh = 128
            scale = dh ** -0.5
            PTr = k.ring("PT", 6, [128, 512], BF16)
            accr = k.ring("acc", 2, [128, 512], F32)
            tpr = k.ring("tp", 2, [128, 512], BF16)
            a1r = k.ring("a1", 2, [128, 512], F32)
            r1r = k.ring("r1", 2, [128, 512], F32)
            ost = k.ring("ost", 3, [128, 512], BF16, dma=True)
            sset = Ring([0, 1, 2])
            hd = {}
            kv = {}

            def load_head(h):
                g = h // 4
                if h % 4 == 0:
                    kt, vt = KTr.next(), VTr.next()
                    k.dma("sp", kt.t[:], self.KTd[g * 128:(g + 1) * 128, :], writes=[kt.b])
                    vsrc = self.VH[g].rearrange("(n p) d -> p n d", p=128)
                    for a in range(0, NT, 22):
                        k.dma("sp", vt.t[:, a:min(a + 22, NT), :], vsrc[:, a:min(a + 22, NT), :], writes=[vt.b])
                    kv[g] = (kt, vt)
                qt = QTr.next()
                k.dma("sp", qt.t[:], self.QT[h * 128:(h + 1) * 128, :], writes=[qt.b])
                hd[h] = (kv[g][0], kv[g][1], qt)

            heads = list(range(8))
            units = []
            for h in heads:
                for (q0, n, nkt) in qblocks:
                    for kt in range(nkt):
                        units.append((h, q0, n, nkt, kt))

            def S(u, d):
                h, q0, n, nkt, kti = u
                kt, vt, qt = hd[h]
                s = sset.next()
                d["s"] = s
                k.mm_group(ps[:, s, 0:n], [(kt.t[:, kti * 128:(kti + 1) * 128], qt.t[:, q0:q0 + n])],
                           reads=[kt.b, qt.b], writes=[k.pb[s]])

            def X(u, d):
                h, q0, n, nkt, kti = u
                s = d["s"]
                pt = PTr.next()
                d["pt"] = pt
                k.op("act", lambda e: e.activation(out=pt.t[:, 0:n], in_=ps[:, s, 0:n], func=AF.Exp, scale=scale),
                     reads=[k.pb[s]], writes=[pt.b])

            qst = {}

            def PV(u, d):
                h, q0, n, nkt, kti = u
                kt, vt, qt = hd[h]
                pt = d["pt"]
                st, sp_ = (kti == 0), (kti == nkt - 1)
                k.mm_group(ps[:, 4, 0:n], [(vt.t[:, kti, :], pt.t[:, 0:n])], reads=[vt.b, pt.b], writes=[k.pb[4]],
                           first=st, last=sp_)
                if st:
                    qst["acc"] = accr.next()
                acc = qst["acc"]
                if kti % 2 == 0:
                    qst["prev"] = pt
                else:
                    pa = qst["prev"]
                    if kti == 1:
                        k.op("dve", lambda e: e.tensor_tensor(out=acc.t[:, 0:n], in0=pa.t[:, 0:n], in1=pt.t[:, 0:n],
                                                              op=ALU.add), reads=[pa.b, pt.b], writes=[acc.b])
                    else:
                        tp = tpr.next()
                        k.op("dve", lambda e: e.tensor_tensor(out=tp.t[:, 0:n], in0=pa.t[:, 0:n], in1=pt.t[:, 0:n],
                                                              op=ALU.add), reads=[pa.b, pt.b], writes=[tp.b])
                        k.op("dve", lambda e: e.tensor_tensor(out=acc.t[:, 0:n], in0=acc.t[:, 0:n], in1=tp.t[:, 0:n],
                                                              op=ALU.add), reads=[acc.b, tp.b], writes=[acc.b])
                if not sp_:
                    return None
                a1 = a1r.next()
                k.op("dve", lambda e: e.tensor_copy(out=a1.t[:, 0:n], in_=ps[:, 4, 0:n]), reads=[k.pb[4]],
                     writes=[a1.b])

                def partB():
                    k.mm_group(ps[:, 6, 0:n], [(self.ones_f.t[:], acc.t[:, 0:n])], reads=[self.ones_f.b, acc.b],
                               writes=[k.pb[6]])
                    r1, o = r1r.next(), ost.next()
                    k.op("dve", lambda e: e.reciprocal(out=r1.t[:, 0:n], in_=ps[:, 6, 0:n]), reads=[k.pb[6]],
                         writes=[r1.b])
                    k.op("dve", lambda e: e.tensor_tensor(out=o.t[:, 0:n], in0=a1.t[:, 0:n], in1=r1.t[:, 0:n],
                                                          op=ALU.mult), reads=[a1.b, r1.b], writes=[o.b])
                    k.dma("sp", self.OT[h * 128:(h + 1) * 128, q0:q0 + n], o.t[:, 0:n], reads=[o.b])
                return partB

        load_head(heads[0])
        load_head(heads[1])
        ds = [dict() for _ in units]
        pend = []
        for i, u in enumerate(units):
            S(u, ds[i])
            X(u, ds[i])
            if i >= 1:
                pb_ = PV(units[i - 1], ds[i - 1])
                ds[i - 1] = None
                for p in pend:
                    p[0] -= 1
                while pend and pend[0][0] <= 0:
                    pend.pop(0)[1]()
                if pb_ is not None:
                    pend.append([2, pb_])
                if units[i - 1][0] != u[0] and u[0] + 1 < 8:
                    load_head(u[0] + 1)
        pb_ = PV(units[-1], ds[-1])
        while pend:
            pend.pop(0)[1]()
        if pb_ is not None:
            pb_()
        k.phase_end()

    def p3(self, L, kind, last):
        k, W = self.k, self.W
        j = L // 3
        ps = k.ps
        NB = 256
        k.phase_begin()
        w_out_src = {0: "diff_w_out", 1: "gqa_w_out", 2: "gla_w_out"}[kind]
        wo = k.tb("wo", [128, KC, D], BF16, dma=True)
        wgu = k.tb("wgu", [128, KC, 2 * FH], BF16, dma=True)
        wd = k.tb("wd", [128, FC, D], BF16, dma=True)
        for c in range(KC):
            k.dma("pool", wo.t[:, c, :], W[w_out_src][j][c * 128:(c + 1) * 128, :], writes=[wo.b])
        for c in range(KC):
            k.dma("pool", wgu.t[:, c, :], W["ffn_w_gu"][L][c * 128:(c + 1) * 128, :], writes=[wgu.b])
        for c in range(FC):
            k.dma("pool", wd.t[:, c, :], W["ffn_w_down"][L][c * 128:(c + 1) * 128, :], writes=[wd.b])
        env = self.norm_env(NB)
        xtr = k.ring("xt", 2, [128, KC, NB], F32, dma=True)
        obr = k.ring("ob", 1, [128, KC, NB], BF16, dma=True) if kind != 2 else None
        if kind == 2:
            gtl = {
                "of": k.ring("gof", 2, [128, 2, NB], BF16, dma=True),
                "ob": k.ring("gob", 2, [128, 2, NB], BF16, dma=True),
                "rt": k.ring("grt", 2, [128, 2, NB], BF16, dma=True),
                "s": k.ring("gs", 2, [128, NB], F32),
                "sq": k.ring("gsq", 2, [128, NB], BF16),
                "ln": k.ring("gln", 1, [128, NB], F32),
                "rs": k.ring("grs", 1, [128, NB], F32),
                "gain": k.tb("ggain", [128, 2], F32, dma=True),
            }
            k.dma("sp", gtl["gain"].t[:], W["gla_out_norm"][0].rearrange("(e p) -> p e", p=128),
                  writes=[gtl["gain"].b], allow_slow_non_contiguous=True)
        hxr = k.ring("hx", 2, [128, KC, NB], BF16)
        actr = k.ring("actT", 1, [128, FC, NB], BF16)
        sgr = k.ring("sg", 4, [128, NB], F32)
        if kind == 0:
            lambda_init = 0.8 - 0.6 * math.exp(-0.3 * L)
            sln = k.tb("sln", [128, 1], F32, dma=True)
            k.dma("sp", sln.t[:, 0:1], W["diff_subln"][j].rearrange("(p o) -> p o", o=1), writes=[sln.b],
                  allow_slow_non_contiguous=True)
            k.op("dve", lambda e: e.tensor_scalar(out=sln.t[:], in0=sln.t[:], scalar1=1.0 - lambda_init,
                                                  scalar2=None, op0=ALU.mult), reads=[sln.b], writes=[sln.b])
            osq = k.ring("osq", 1, [128, NB], BF16)
            oln = k.ring("oln", 1, [128, NB], F32)
            ors = k.ring("ors", 1, [128, NB], F32)
        if last:
            ost = k.ring("fin", 1, [128, D], F32, dma=True)
        blocks = ([] if last else [(0, NCTX, 1)]) + [(NCTX + NB * i, NB, 0) for i in range(SEQ // NB)]
        pj = Ring([1, 2, 3, 4, 5, 6, 7])
        state = {}

        def load(bi):
            t0, n, v = blocks[bi]
            xt = xtr.next()
            k.dma("sp", xt.t[:, :, 0:n], self.xt_r.rearrange("(c p) t -> p c t", p=128)[:, :, t0:t0 + n],
                  writes=[xt.b])
            state[bi] = {"xt": xt}

        obst = {}

        def load_ob(bi):
            t0, n, v = blocks[bi]
            ob = obr.next() if obr is not None else None
            if kind in (0, 1):
                k.dma("sp", ob.t[:, :, 0:n], self.OT.rearrange("(c p) t -> p c t", p=128)[:, :, t0:t0 + n],
                      writes=[ob.b])
            obst[bi] = ob

        def pre_steps(bi):
            blk = blocks[bi]
            t0, n, v = blk
            st = state[bi]
            xt = st["xt"]
            steps = []

            def begin():
                st["hx"] = hxr.next()
                st["on"] = obst[bi] if kind == 1 else st["hx"]
            steps.append(begin)
            if kind == 0:
                for c in range(KC):
                    def chain(c=c):
                        ob, on = obst[bi], st["on"]
                        sq, ln, rs = osq.next(), oln.next(), ors.next()
                        b = pj.next()
                        k.op("pool", lambda e: e.tensor_tensor(out=sq.t[:, 0:n], in0=ob.t[:, c, 0:n],
                                                               in1=ob.t[:, c, 0:n], op=ALU.mult),
                             reads=[ob.b], writes=[sq.b])
                        k.mm_group(ps[:, b, 0:n], [(self.ones_b.t[:], sq.t[:, 0:n])], reads=[self.ones_b.b, sq.b],
                                   writes=[k.pb[b]])
                        k.op("act", lambda e: e.activation(out=ln.t[:, 0:n], in_=ps[:, b, 0:n], func=AF.Ln,
                                                           scale=1.0 / 128, bias=EPS), reads=[k.pb[b]], writes=[ln.b])
                        k.op("act", lambda e: e.activation(out=rs.t[:, 0:n], in_=ln.t[:, 0:n], func=AF.Exp,
                                                           scale=-0.5), reads=[ln.b], writes=[rs.b])
                        k.op("dve", lambda e: e.scalar_tensor_tensor(
                            out=on.t[:, c, 0:n], in0=ob.t[:, c, 0:n], scalar=sln.t[:, 0:1], in1=rs.t[:, 0:n],
                            op0=ALU.mult, op1=ALU.mult), reads=[ob.b, sln.b, rs.b], writes=[on.b])
                    steps.append(chain)
            elif kind == 2:
                for hh in range(4):
                    steps.append(lambda hh=hh: self.gla_mixer_out(L, blk, st["on"], env, pj, gtl, hh))
            if kind == 0 and bi + 1 < len(blocks):
                steps.append(lambda: load_ob(bi + 1))
            for c in range(KC):
                def oproj(c=c):
                    on = st["on"]
                    b = pj.next()
                    k.mm_group(ps[:, b, 0:n], [(wo.t[:, kc, c * 128:(c + 1) * 128], on.t[:, kc, 0:n])
                                               for kc in range(KC)], reads=[wo.b, on.b], writes=[k.pb[b]])
                    k.op("dve", lambda e: e.scalar_tensor_tensor(
                        out=xt.t[:, c, 0:n], in0=ps[:, b, 0:n], scalar=self.mod.t[:, L, 16 + c, v:v + 1],
                        in1=xt.t[:, c, 0:n], op0=ALU.mult, op1=ALU.add), reads=[k.pb[b], self.mod.b, xt.b],
                        writes=[xt.b])
                steps.append(oproj)
            if kind == 1 and bi + 1 < len(blocks):
                steps.append(lambda: load_ob(bi + 1))
            steps.append(lambda: self.norm_part1(env, blk, None, xt, sq=st["hx"]))
            steps.append(lambda: self.norm_part2(env, blk, xt, st["hx"], L, self.g2t, 24, st["hx"], 0))
            return steps

        def ffn(bi, inter):
            t0, n, v = blocks[bi]
            st = state[bi]
            xt = st["xt"]
            h2 = st["hx"]
            act = actr.next()
            for f in range(FC):
                bg, bu = pj.next(), pj.next()
                k.mm_group(ps[:, bg, 0:n], [(wgu.t[:, kc, f * 128:(f + 1) * 128], h2.t[:, kc, 0:n])
                                            for kc in range(KC)], reads=[wgu.b, h2.b], writes=[k.pb[bg]])
                k.mm_group(ps[:, bu, 0:n], [(wgu.t[:, kc, FH + f * 128:FH + (f + 1) * 128], h2.t[:, kc, 0:n])
                                            for kc in range(KC)], reads=[wgu.b, h2.b], writes=[k.pb[bu]])
                eg, rg = sgr.next(), sgr.next()
                k.op("act", lambda e: e.activation(out=eg.t[:, 0:n], in_=ps[:, bg, 0:n], func=AF.Exp, scale=-1.0),
                     reads=[k.pb[bg]], writes=[eg.b])
                k.op("dve", lambda e: e.tensor_scalar(out=eg.t[:, 0:n], in0=eg.t[:, 0:n], scalar1=1.0, scalar2=None,
                                                      op0=ALU.add), reads=[eg.b], writes=[eg.b])
                k.op("dve", lambda e: e.reciprocal(out=rg.t[:, 0:n], in_=eg.t[:, 0:n]), reads=[eg.b], writes=[rg.b])
                k.op("dve", lambda e: e.tensor_tensor(out=rg.t[:, 0:n], in0=ps[:, bu, 0:n], in1=rg.t[:, 0:n],
                                                      op=ALU.mult), reads=[k.pb[bu], rg.b], writes=[rg.b])
                k.op("dve", lambda e: e.tensor_tensor(out=act.t[:, f, 0:n], in0=ps[:, bg, 0:n], in1=rg.t[:, 0:n],
                                                      op=ALU.mult), reads=[k.pb[bg], rg.b], writes=[act.b])
                if inter:
                    inter.pop(0)()
            while inter:
                inter.pop(0)()
            for c in range(KC):
                b = pj.next()
                k.mm_group(ps[:, b, 0:n], [(wd.t[:, f, c * 128:(c + 1) * 128], act.t[:, f, 0:n]) for f in range(FC)],
                           reads=[wd.b, act.b], writes=[k.pb[b]])
                k.op("dve", lambda e, b=b, c=c: e.scalar_tensor_tensor(
                    out=xt.t[:, c, 0:n], in0=ps[:, b, 0:n], scalar=self.mod.t[:, L, 40 + c, v:v + 1],
                    in1=xt.t[:, c, 0:n], op0=ALU.mult, op1=ALU.add), reads=[k.pb[b], self.mod.b, xt.b],
                    writes=[xt.b])
            if last and self.final:
                E = k.E["pe"]
                for ti in range(n // 128):
                    fo = ost.next()
                    for half in range(2):
                        b = pj.next()
                        k._wait(E, k._deps([xt.b, self.ident_f.b], [k.pb[b]]))
                        inst = None
                        for cc in range(4):
                            c = half * 4 + cc
                            inst = E.e.transpose(ps[:, b, cc * 128:(cc + 1) * 128],
                                                 xt.t[:, c, ti * 128:(ti + 1) * 128], self.ident_f.t[:])
                        E.cnt += 1
                        inst.then_inc(E.sem, 1)
                        k._record(Ev(E, E.cnt), [xt.b], [k.pb[b]])
                        k.op("act", lambda e, b=b, half=half, fo=fo: e.activation(
                            out=fo.t[:, half * 512:(half + 1) * 512], in_=ps[:, b, 0:512], func=AF.Copy),
                            reads=[k.pb[b]], writes=[fo.b])
                    r0 = t0 - NCTX + ti * 128
                    k.dma("sp", self.OUT[r0:r0 + 128, :], fo.t[:], reads=[fo.b])
            else:
                k.dma("sp", self.XTw.rearrange("(c p) t -> p c t", p=128)[:, :, t0:t0 + n], xt.t[:, :, 0:n],
                      reads=[xt.b])

        load(0)
        load_ob(0)
        for s_ in pre_steps(0):
            s_()
        for bi in range(len(blocks)):
            if bi + 1 < len(blocks):
                load(bi + 1)
            ffn(bi, pre_steps(bi + 1) if bi + 1 < len(blocks) else [])
        k.phase_end()

    def p1_gla(self, L):
        k, W = self.k, self.W
        ps = k.ps
        nc = self.nc
        if not hasattr(self, "GK"):
            self.GK = nc.dram_tensor("GK", [T, 512], BF16).ap()
            self.GV = nc.dram_tensor("GV", [T, 1024], BF16).ap()
            self.GG = nc.dram_tensor("GG", [2, T, 512], F32).ap()
            self.OT2 = nc.dram_tensor("OT2", [D, T], BF16).ap()
        k.phase_begin()
        win = k.tb("win", [128, KC, 3072], BF16, dma=True)
        for c in range(KC):
            k.dma("pool", win.t[:, c, :], W["gla_w_in"][0][c * 128:(c + 1) * 128, :], writes=[win.b])
        w1 = k.tb("w1", [128, KC, 32], BF16, dma=True)
        for d, nm in enumerate(("gla_gate_w1_fwd", "gla_gate_w1_bwd")):
            k.dma("pool", w1.t[:, :, d * 16:(d + 1) * 16], W[nm][0].rearrange("(c p) r -> p c r", p=128),
                  writes=[w1.b], allow_slow_non_contiguous=True)
        w2e = k.tb("w2e", [64, 2, 512], F32, dma=True)
        k.op("dve", lambda e: e.memset(w2e.t[:], 0.0), writes=[w2e.b])
        for d, (nw, nb) in enumerate((("gla_gate_w2_fwd", "gla_gate_b_fwd"), ("gla_gate_w2_bwd", "gla_gate_b_bwd"))):
            k.dma("sp", w2e.t[0:16, d, :], W[nw][0], writes=[w2e.b])
            k.dma("sp", w2e.t[32:33, d, :], W[nb][0].rearrange("(o n) -> o n", o=1), writes=[w2e.b])
        ut = [k.tb("ut%d" % d, [64, 512], F32) for d in range(2)]
        for d in range(2):
            k.op("dve", lambda e, d=d: e.memset(ut[d].t[:], 0.0), writes=[ut[d].b])
            k.op("dve", lambda e, d=d: e.memset(ut[d].t[32:33, :], 1.0), writes=[ut[d].b])
        env = self.norm_env(512)
        xtr = k.ring("xt", 2, [128, KC, 512], F32, dma=True)
        hTr = k.ring("hT", 2, [128, KC, 512], BF16)
        fmo = k.ring("fmo", 3, [128, 512], BF16, dma=True)
        tmo = k.ring("tmo", 3, [128, 512], BF16, dma=True)
        ger = k.ring("ge", 2, [128, 512], F32)
        glr = k.ring("gl", 2, [128, 512], F32)
        ggr = k.ring("gg", 2, [128, 512], F32, dma=True)
        pj = Ring([1, 2, 3, 4, 5, 6, 7])
        blocks = [(0, NCTX, 1)] + [(NCTX + 512 * i, 512, 0) for i in range(SEQ // 512)]
        qscale = 128 ** -0.5
        cp = [0]
        jobs = []
        for bi, blk in enumerate(blocks):
            t0, n, v = blk
            st = {}

            def norm(blk=blk, st=st):
                xt = xtr.next()
                sq = self.norm_part1(env, blk, self.xt_r, xt)
                hT = hTr.next()
                self.norm_part2(env, blk, xt, sq, L, self.g1t, 0, hT, 0)
                st["hT"] = hT
            jobs.append([norm])
            for ci in range(16):
                d = {}
                col0 = ci * 128 if ci < 8 else 2048 + (ci - 8) * 128

                def A(d=d, col0=col0, st=st, n=n):
                    d["b"] = pj.next()
                    k.mm_group(ps[:, d["b"], 0:n], [(win.t[:, c, col0:col0 + 128], st["hT"].t[:, c, 0:n])
                                                    for c in range(KC)], reads=[win.b, st["hT"].b],
                               writes=[k.pb[d["b"]]])

                def B(d=d, ci=ci, n=n, t0=t0):
                    o = fmo.next()
                    if ci < 4:
                        k.op("act", lambda e: e.activation(out=o.t[:, 0:n], in_=ps[:, d["b"], 0:n], func=AF.Copy,
                                                           scale=qscale), reads=[k.pb[d["b"]]], writes=[o.b])
                        dst = self.QT[ci * 128:(ci + 1) * 128, t0:t0 + n]
                    elif ci < 8:
                        k.op("dve", lambda e: e.tensor_copy(out=o.t[:, 0:n], in_=ps[:, d["b"], 0:n]),
                             reads=[k.pb[d["b"]]], writes=[o.b])
                        dst = self.QT[ci * 128:(ci + 1) * 128, t0:t0 + n]
                    else:
                        k.op("act", lambda e: e.activation(out=o.t[:, 0:n], in_=ps[:, d["b"], 0:n], func=AF.Silu),
                             reads=[k.pb[d["b"]]], writes=[o.b])
                        dst = self.KTd[(ci - 8) * 128:(ci - 7) * 128, t0:t0 + n]
                    k.dma("sp", dst, o.t[:, 0:n], reads=[o.b])
                jobs.append([A, B])
            for ti in range(n // 128):
                for cg in range(3):
                    d = {}

                    def A(d=d, ti=ti, cg=cg, st=st):
                        d["b"] = pj.next()
                        c0 = 512 + cg * 512
                        k.mm_group(ps[:, d["b"], 0:512],
                                   [(st["hT"].t[:, c, ti * 128:(ti + 1) * 128], win.t[:, c, c0:c0 + 512])
                                    for c in range(KC)], reads=[win.b, st["hT"].b], writes=[k.pb[d["b"]]])

                    def B(d=d, ti=ti, cg=cg, t0=t0):
                        o = tmo.next()
                        cp[0] += 1
                        if cp[0] % 2:
                            k.op("act", lambda e: e.activation(out=o.t[:], in_=ps[:, d["b"], 0:512], func=AF.Copy),
                                 reads=[k.pb[d["b"]]], writes=[o.b])
                        else:
                            k.op("dve", lambda e: e.tensor_copy(out=o.t[:], in_=ps[:, d["b"], 0:512]),
                                 reads=[k.pb[d["b"]]], writes=[o.b])
                        tt = t0 + ti * 128
                        dst = self.GK[tt:tt + 128, :] if cg == 0 else self.GV[tt:tt + 128, (cg - 1) * 512:cg * 512]
                        k.dma("sp", dst, o.t[:], reads=[o.b])
                    jobs.append([A, B])
            for dd in range(2):
                d = {}

                def A(d=d, dd=dd, st=st, n=n):
                    d["b"] = pj.next()
                    k.mm_group(ps[0:16, d["b"], 0:n], [(w1.t[:, c, dd * 16:(dd + 1) * 16], st["hT"].t[:, c, 0:n])
                                                       for c in range(KC)], reads=[w1.b, st["hT"].b],
                               writes=[k.pb[d["b"]]])

                def B(d=d, dd=dd, n=n):
                    k.op("dve", lambda e: e.tensor_copy(out=ut[dd].t[0:16, 0:n], in_=ps[0:16, d["b"], 0:n]),
                         reads=[k.pb[d["b"]]], writes=[ut[dd].b])
                jobs.append([A, B])
            for ti in range(n // 128):
                for dd in range(2):
                    d = {}

                    def A(d=d, dd=dd, ti=ti):
                        d["b"] = pj.next()
                        k.mm_group(ps[:, d["b"], 0:512],
                                   [(ut[dd].t[0:33, ti * 128:(ti + 1) * 128], w2e.t[0:33, dd, :])],
                                   reads=[ut[dd].b, w2e.b], writes=[k.pb[d["b"]]])

                    def B(d=d, dd=dd, ti=ti, t0=t0):
                        ge, gl, gg = ger.next(), glr.next(), ggr.next()
                        k.op("act", lambda e: e.activation(out=ge.t[:], in_=ps[:, d["b"], 0:512], func=AF.Exp,
                                                           scale=-1.0), reads=[k.pb[d["b"]]], writes=[ge.b])
                        k.op("act", lambda e: e.activation(out=gl.t[:], in_=ge.t[:], func=AF.Ln, bias=1.0),
                             reads=[ge.b], writes=[gl.b])
                        k.op("dve", lambda e: e.tensor_scalar(out=gg.t[:], in0=gl.t[:], scalar1=-1.0 / 16.0,
                                                              scalar2=None, op0=ALU.mult), reads=[gl.b], writes=[gg.b])
                        tt = t0 + ti * 128
                        k.dma("sp", self.GG[dd, tt:tt + 128, :], gg.t[:], reads=[gg.b])
                    jobs.append([A, B])
        run_pipeline(jobs)
        k.phase_end()

    def p2_gla(self, L, last):
        k, W = self.k, self.W
        ps = k.ps
        k.phase_begin()
        NCH = T // 64
        SEGC = 4
        tri = k.tb("tri", [64, 4, 64], F32, dma=True)
        k.dma("sp", tri.t[:], W["c_gla_tri"].rearrange("p (a t) -> p a t", a=4), writes=[tri.b])
        S = k.tb("S", [128, 8, 256], F32)
        Sb = k.tb("Sb", [128, 8, 256], BF16)
        Sbuf = [Buf("S%d" % i) for i in range(8)]
        Sbb = [Buf("Sb%d" % i) for i in range(8)]
        k.op("dve", lambda e: e.memset(S.t[:], 0.0), writes=Sbuf)
        k.op("dve", lambda e: e.memset(Sb.t[:], 0.0), writes=Sbb)
        gr = [k.ring("g%d" % d, 2, [64, SEGC, 512], F32, dma=True) for d in range(2)]
        kkr = [k.ring("kk%d" % d, 2, [64, SEGC, 512], BF16, dma=True) for d in range(2)]
        vvr = [k.ring("vv%d" % d, 2, [64, SEGC, 1024], BF16, dma=True) for d in range(2)]
        qTr = [k.ring("qT%d" % d, 2, [128, 8, SEGC * 64], BF16, dma=True) for d in range(2)]
        osr = [k.ring("os%d" % d, 2, [128, 8, SEGC * 64], BF16, dma=True) for d in range(2)]
        Epr = k.ring("Ep", 16, [128, 64], F32)
        Enr = k.ring("En", 16, [128, 64], F32)
        Err = k.ring("Er", 16, [64, 128], F32)
        qer = k.ring("qe", 16, [128, 64], BF16)
        ker = k.ring("keT", 16, [128, 64], BF16)
        k2r = k.ring("ke2", 16, [64, 128], BF16)
        Amr = k.ring("Am", 16, [64, 64], BF16)
        order = [list(range(NCH)), [3, 2, 1, 0] + list(range(NCH - 1, 3, -1))]
        cur = [None, None]
        osc = [None, None]

        def seg_of(c):
            return c // SEGC

        def load_seg(d, sg):
            t0 = sg * SEGC * 64
            nt = SEGC * 64
            g, kk, vv, qT = gr[d].next(), kkr[d].next(), vvr[d].next(), qTr[d].next()
            k.dma("sp", g.t[:], self.GG[d, t0:t0 + nt, :].rearrange("(c p) f -> p c f", p=64), writes=[g.b])
            k.dma("sp", kk.t[:], self.GK[t0:t0 + nt, :].rearrange("(c p) f -> p c f", p=64), writes=[kk.b])
            k.dma("sp", vv.t[:], self.GV[t0:t0 + nt, :].rearrange("(c p) f -> p c f", p=64), writes=[vv.b])
            k.dma("sp", qT.t[:], self.QT.rearrange("(c p) t -> p c t", p=128)[:, :, t0:t0 + nt], writes=[qT.b])
            return (sg, g, kk, vv, qT)

        def flush_os(d):
            sg, o = osc[d]
            t0 = sg * SEGC * 64
            dst = (self.OT if d == 0 else self.OT2).rearrange("(c p) t -> p c t", p=128)[:, :, t0:t0 + SEGC * 64]
            k.dma("sp", dst, o.t[:], reads=[o.b])

        nxt = [None, None]
        for d in range(2):
            cur[d] = load_seg(d, seg_of(order[d][0]))
        for step in range(NCH):
            units = []
            for d in range(2):
                c = order[d][step]
                sg = seg_of(c)
                if cur[d][0] != sg:
                    cur[d] = nxt[d] if (nxt[d] is not None and nxt[d][0] == sg) else load_seg(d, sg)
                    nxt[d] = None
                if osc[d] is None or osc[d][0] != sg:
                    if osc[d] is not None:
                        flush_os(d)
                    osc[d] = (sg, osr[d].next())
                for hh in range(4):
                    units.append((d, hh, c))
            sts = [dict() for _ in units]
            for u, st in zip(units, sts):
                d, hh, c = u
                sg, g, kk, vv, qT = cur[d]
                ci = c - sg * SEGC
                bnk = d * 4 + hh
                st["bnk"] = bnk
                E = k.E["pe"]
                gsl = g.t[:, ci, hh * 128:(hh + 1) * 128]
                k._wait(E, k._deps([g.b, tri.b], [k.pb[bnk]]))
                E.e.matmul(ps[:, bnk, 0:64], gsl, tri.t[:, d, :], start=True, stop=True)
                inst = E.e.matmul(ps[0:64, bnk, 64:192], tri.t[:, 2 + d, :], gsl, start=True, stop=True)
                E.cnt += 1
                inst.then_inc(E.sem, 1)
                k._record(Ev(E, E.cnt), [g.b, tri.b], [k.pb[bnk]])
                Ep, En, Er = Epr.next(), Enr.next(), Err.next()
                qe, keT, ke2 = qer.next(), ker.next(), k2r.next()
                st.update(Ep=Ep, qe=qe, keT=keT, ke2=ke2)
                k.op("act", lambda e: e.activation(out=Ep.t[:], in_=ps[:, bnk, 0:64], func=AF.Exp),
                     reads=[k.pb[bnk]], writes=[Ep.b])
                k.op("act", lambda e: e.activation(out=En.t[:], in_=ps[:, bnk, 0:64], func=AF.Exp, scale=-1.0),
                     reads=[k.pb[bnk]], writes=[En.b])
                k.op("act", lambda e: e.activation(out=Er.t[:], in_=ps[0:64, bnk, 64:192], func=AF.Exp),
                     reads=[k.pb[bnk]], writes=[Er.b])
                cs = slice(ci * 64, (ci + 1) * 64)
                k.op("dve", lambda e: e.tensor_tensor(out=qe.t[:], in0=qT.t[:, hh, cs], in1=Ep.t[:], op=ALU.mult),
                     reads=[qT.b, Ep.b], writes=[qe.b])
                k.op("dve", lambda e: e.tensor_tensor(out=keT.t[:], in0=qT.t[:, 4 + hh, cs], in1=En.t[:],
                                                      op=ALU.mult), reads=[qT.b, En.b], writes=[keT.b])
                k.op("dve", lambda e: e.tensor_tensor(out=ke2.t[:], in0=kk.t[:, ci, hh * 128:(hh + 1) * 128],
                                                      in1=Er.t[:], op=ALU.mult), reads=[kk.b, Er.b], writes=[ke2.b])
            for u, st in zip(units, sts):
                d, hh, c = u
                bnk = st["bnk"]
                k.mm_group(ps[0:64, bnk, 192:256], [(st["keT"].t[:], st["qe"].t[:])],
                           reads=[st["keT"].b, st["qe"].b], writes=[k.pb[bnk]])
                Am = Amr.next()
                st["Am"] = Am
                k.op("dve", lambda e: e.tensor_tensor(out=Am.t[:], in0=ps[0:64, bnk, 192:256], in1=tri.t[:, d, :],
                                                      op=ALU.mult), reads=[k.pb[bnk], tri.b], writes=[Am.b])
            for u, st in zip(units, sts):
                d, hh, c = u
                sg, g, kk, vv, qT = cur[d]
                ci = c - sg * SEGC
                bnk = st["bnk"]
                ch = d * 4 + hh
                E = k.E["pe"]
                k._wait(E, k._deps([Sbb[ch], st["qe"].b, vv.b, st["Am"].b, st["ke2"].b], [k.pb[bnk]]))
                for e2 in range(2):
                    E.e.matmul(ps[:, bnk, 256 + e2 * 64:256 + (e2 + 1) * 64], Sb.t[:, ch, e2 * 128:(e2 + 1) * 128],
                               st["qe"].t[:], start=True, stop=False)
                    inst = E.e.matmul(ps[:, bnk, 256 + e2 * 64:256 + (e2 + 1) * 64],
                                      vv.t[:, ci, hh * 256 + e2 * 128:hh * 256 + (e2 + 1) * 128], st["Am"].t[:],
                                      start=False, stop=True)
                E.cnt += 1
                inst.then_inc(E.sem, 1)
                k._record(Ev(E, E.cnt), [Sbb[ch], st["qe"].b, vv.b, st["Am"].b], [k.pb[bnk]])
                sgo, o = osc[d]
                k.op("act", lambda e: e.activation(
                    out=o.t[:, 2 * hh:2 * hh + 2, ci * 64:(ci + 1) * 64],
                    in_=ps[:, bnk, 256:384].rearrange("p (a t) -> p a t", a=2), func=AF.Copy),
                    reads=[k.pb[bnk]], writes=[o.b])
                k.mm_group(ps[:, bnk, 0:256], [(st["ke2"].t[:], vv.t[:, ci, hh * 256:(hh + 1) * 256])],
                           reads=[st["ke2"].b, vv.b], writes=[k.pb[bnk]])
                dcol = 63 if d == 0 else 0
                k.op("dve", lambda e: e.scalar_tensor_tensor(
                    out=S.t[:, ch, :], in0=S.t[:, ch, :], scalar=st["Ep"].t[:, dcol:dcol + 1], in1=ps[:, bnk, 0:256],
                    op0=ALU.mult, op1=ALU.add), reads=[Sbuf[ch], st["Ep"].b, k.pb[bnk]], writes=[Sbuf[ch]])
                k.op("pool", lambda e: e.tensor_copy(out=Sb.t[:, ch, :], in_=S.t[:, ch, :]), reads=[Sbuf[ch]],
                     writes=[Sbb[ch]])
            if step + 1 < NCH:
                for d in range(2):
                    c2 = order[d][step + 1]
                    for la in range(step + 1, min(step + 1 + SEGC, NCH)):
                        sg2 = seg_of(order[d][la])
                        if sg2 != cur[d][0]:
                            if nxt[d] is None:
                                nxt[d] = load_seg(d, sg2)
                            break
        for d in range(2):
            flush_os(d)
        k.phase_end()

    def gla_mixer_out(self, L, blk, on, env, pj, tl, hh_only):
        k = self.k
        ps = k.ps
        t0, n, v = blk
        for hh in (hh_only,):
            of, ob, rt = tl["of"].next(), tl["ob"].next(), tl["rt"].next()
            rows = slice(hh * 256, (hh + 1) * 256)
            k.dma("sp", of.t[:, :, 0:n], self.OT[rows, t0:t0 + n].rearrange("(c p) t -> p c t", p=128), writes=[of.b])
            k.dma("sp", ob.t[:, :, 0:n], self.OT2[rows, t0:t0 + n].rearrange("(c p) t -> p c t", p=128), writes=[ob.b])
            k.dma("sp", rt.t[:, :, 0:n], self.KTd[rows, t0:t0 + n].rearrange("(c p) t -> p c t", p=128), writes=[rt.b])
            ss = []
            b = pj.next()
            for e2 in range(2):
                s_, sq = tl["s"].next(), tl["sq"].next()
                ss.append(s_)
                k.op("dve", lambda e, e2=e2, s_=s_: e.tensor_tensor(out=s_.t[:, 0:n], in0=of.t[:, e2, 0:n],
                                                                    in1=ob.t[:, e2, 0:n], op=ALU.add),
                     reads=[of.b, ob.b], writes=[s_.b])
                k.op("pool", lambda e, s_=s_, sq=sq: e.tensor_tensor(out=sq.t[:, 0:n], in0=s_.t[:, 0:n],
                                                                     in1=s_.t[:, 0:n], op=ALU.mult),
                     reads=[s_.b], writes=[sq.b])
                k.mm_group(ps[:, b, 0:n], [(self.ones_b.t[:], sq.t[:, 0:n])], reads=[self.ones_b.b, sq.b],
                           writes=[k.pb[b]], first=(e2 == 0), last=(e2 == 1))
            ln, rs = tl["ln"].next(), tl["rs"].next()
            k.op("act", lambda e: e.activation(out=ln.t[:, 0:n], in_=ps[:, b, 0:n], func=AF.Ln, scale=1.0 / 256,
                                               bias=EPS), reads=[k.pb[b]], writes=[ln.b])
            k.op("act", lambda e: e.activation(out=rs.t[:, 0:n], in_=ln.t[:, 0:n], func=AF.Exp, scale=-0.5),
                 reads=[ln.b], writes=[rs.b])
            for e2 in range(2):
                s_ = ss[e2]
                k.op("dve", lambda e, e2=e2, s_=s_: e.scalar_tensor_tensor(
                    out=s_.t[:, 0:n], in0=s_.t[:, 0:n], scalar=tl["gain"].t[:, e2:e2 + 1], in1=rs.t[:, 0:n],
                    op0=ALU.mult, op1=ALU.mult), reads=[s_.b, tl["gain"].b, rs.b], writes=[s_.b])
                k.op("dve", lambda e, e2=e2, s_=s_: e.tensor_tensor(
                    out=on.t[:, 2 * hh + e2, 0:n], in0=s_.t[:, 0:n], in1=rt.t[:, e2, 0:n], op=ALU.mult),
                    reads=[s_.b, rt.b], writes=[on.b])


WEIGHT_SHAPES = {
    "ada_w": (4, 1024, 6144), "ada_b": (4, 6144), "norm1_g": (4, 1024), "norm2_g": (4, 1024),
    "ffn_w_gu": (4, 1024, 5632), "ffn_w_down": (4, 2816, 1024),
    "diff_w_in": (2, 1024, 3072), "diff_w_out": (2, 1024, 1024), "diff_q_norm": (2, 64), "diff_k_norm": (2, 64),
    "diff_lambda_q1": (2, 64), "diff_lambda_k1": (2, 64), "diff_lambda_q2": (2, 64), "diff_lambda_k2": (2, 64),
    "diff_subln": (2, 128),
    "gqa_w_in": (1, 1024, 1536), "gqa_w_out": (1, 1024, 1024), "gqa_q_norm": (1, 128), "gqa_k_norm": (1, 128),
    "gla_w_in": (1, 1024, 3072), "gla_gate_w1_fwd": (1, 1024, 16), "gla_gate_w2_fwd": (1, 16, 512),
    "gla_gate_b_fwd": (1, 512), "gla_gate_w1_bwd": (1, 1024, 16), "gla_gate_w2_bwd": (1, 16, 512),
    "gla_gate_b_bwd": (1, 512), "gla_out_norm": (1, 256), "gla_w_out": (1, 1024, 1024),
}
CONST_SHAPES = {
    "c_ident": (128, 128), "c_bd64": (128, 128), "c_r64t": (128, 128), "c_r128t": (128, 128),
    "c_cos64": (128, T), "c_sin64": (128, T), "c_cos128": (128, T), "c_sin128": (128, T),
    "c_gla_tri": (64, 256),
}


def _needed(layers):
    need = {"ada_w", "ada_b", "norm1_g", "norm2_g", "ffn_w_gu", "ffn_w_down", "c_ident", "c_bd64", "c_r64t",
            "c_r128t"}
    for L in layers:
        kind = L % 3
        if kind == 0:
            need |= {n for n in WEIGHT_SHAPES if n.startswith("diff_")} | {"c_cos64", "c_sin64"}
        elif kind == 1:
            need |= {n for n in WEIGHT_SHAPES if n.startswith("gqa_")} | {"c_cos128", "c_sin128"}
        else:
            need |= {n for n in WEIGHT_SHAPES if n.startswith("gla_")} | {n for n in CONST_SHAPES if n.startswith("c_gla")}
    return need


def _rope_tables(dh):
    half = dh // 2
    inv = (10000.0 ** (-np.arange(0, half, 2, dtype=np.float32) / np.float32(half))).astype(np.float32)
    t = np.arange(SEQ)
    row = (t // GRID_W).astype(np.float32)
    col = (t % GRID_W).astype(np.float32)

    def ax(pos):
        a = pos[:, None] * inv[None, :]
        return np.concatenate([a, a], axis=-1)

    ang = np.concatenate([ax(row), ax(col)], axis=-1).astype(np.float32)
    cos = np.concatenate([np.ones((NCTX, dh), np.float32), np.cos(ang)], axis=0)
    sin = np.concatenate([np.zeros((NCTX, dh), np.float32), np.sin(ang)], axis=0)
    rep = 128 // dh
    cosT = np.ascontiguousarray(np.tile(cos.T, (rep, 1))).astype(np.float32)
    sinT = np.ascontiguousarray(np.tile(sin.T, (rep, 1))).astype(np.float32)
    return cosT, sinT


def _rot_t(dh):
    q = dh // 4
    R = np.zeros((128, 128), np.float32)
    for h0 in range(0, 128, dh):
        for half in range(2):
            base = h0 + half * 2 * q
            for i in range(q):
                R[base + q + i, base + i] = -1.0
                R[base + i, base + q + i] = 1.0
    return R


def _consts():
    c = {}
    c["c_ident"] = np.eye(128, dtype=np.float32)
    bd = np.zeros((128, 128), np.float32)
    bd[0:64, 0:64] = 1.0
    bd[64:128, 64:128] = 1.0
    c["c_bd64"] = bd
    c["c_r64t"] = _rot_t(64)
    c["c_r128t"] = _rot_t(128)
    c["c_cos64"], c["c_sin64"] = _rope_tables(64)
    c["c_cos128"], c["c_sin128"] = _rope_tables(128)
    si = np.arange(64)[:, None]
    ti = np.arange(64)[None, :]
    tri = np.stack([(si <= ti), (si >= ti), (si > ti), (si < ti)], axis=1).astype(np.float32)
    c["c_gla_tri"] = np.ascontiguousarray(tri.reshape(64, 256))
    return c


_PROG_CACHE = {}


def _get_prog(layers, first, final):
    key = (tuple(layers), first, final)
    if key not in _PROG_CACHE:
        _PROG_CACHE[key] = Prog(list(layers), first, final)
    return _PROG_CACHE[key]


LAUNCH_PLAN = [[0, 1, 2, 3]]


def kernel(**inputs):
    n = 8
    consts = _consts()
    shared = {nm: np.ascontiguousarray(inputs[nm], dtype=np.float32) for nm in WEIGHT_SHAPES}
    shared.update(consts)
    shared["c_ctx"] = np.ascontiguousarray(inputs["c_ctx"], dtype=np.float32)
    xt = None
    out = None
    for li, layers in enumerate(LAUNCH_PLAN):
        first = li == 0
        final = li == len(LAUNCH_PLAN) - 1
        prog = _get_prog(layers, first, final)
        in_maps = []
        for b in range(n):
            m = dict(shared)
            m["c"] = np.ascontiguousarray(inputs["c"][b], dtype=np.float32)
            if first:
                m["x"] = np.ascontiguousarray(inputs["x"][b], dtype=np.float32)
                m["ctx"] = np.ascontiguousarray(inputs["ctx"][b], dtype=np.float32)
            else:
                m["xt_in"] = xt[b]
            m = {kk: vv for kk, vv in m.items() if kk in prog.inputs}
            in_maps.append(m)
        res = run_bass_kernel_spmd(prog.nc, in_maps, core_ids=list(range(n)))
        if final:
            out = np.stack([np.asarray(r["out"]) for r in res.results], axis=0)
        else:
            xt = [np.asarray(r["xt_out"]) for r in res.results]
    return out.astype(np.float32)
```

```python
import math
from contextlib import ExitStack
import numpy as np
import concourse.bass as bass
import concourse.mybir as mybir
from concourse.bass_utils import run_bass_kernel_spmd

F32 = mybir.dt.float32
BF16 = mybir.dt.bfloat16
AF = mybir.ActivationFunctionType
ALU = mybir.AluOpType

D = 1024
KC = 8
SEQ = 8192
NCTX = 256
T = SEQ + NCTX
NT = T // 128
DEPTH = 4
FH = 2816
FC = FH // 128
EPS = 1e-6
GRID_W = 64
DEBUG = False


class Ev:
    __slots__ = ("key", "cnt")

    def __init__(self, key, cnt):
        self.key, self.cnt = key, cnt


class Eng:
    def __init__(self, name, e, sem):
        self.name, self.e, self.sem = name, e, sem
        self.cnt = 0
        self.seen = {}
        self.pending = []


class DSem:
    def __init__(self, sem):
        self.sem = sem
        self.cnt = 0


class Buf:
    def __init__(self, name, dma=False):
        self.name, self.dsem, self.dma = name, None, dma
        self.w = {}
        self.r = {}


class TB:
    def __init__(self, t, b):
        self.t, self.b = t, b


class Ring:
    def __init__(self, items):
        self.items, self.i = items, 0

    def next(self):
        x = self.items[self.i % len(self.items)]
        self.i += 1
        return x


class K:
    def __init__(self, nc):
        self.nc = nc
        self.root = ExitStack()
        self.scopes = [self.root]
        self.uid = 0
        self.E = {}
        for name, e in (("pe", nc.tensor), ("act", nc.scalar), ("dve", nc.vector),
                        ("pool", nc.gpsimd), ("sp", nc.sync)):
            self.E[name] = Eng(name, e, self.root.enter_context(nc.semaphore("s_" + name)))
        self.bar_sem = self.root.enter_context(nc.semaphore("s_bar"))
        self.bar_cnt = 0
        self.dsem_free = {"sp": [], "pool": []}
        self.dsem_all = []
        self.scope_dsems = [[]]
        self.ps = self.root.enter_context(nc.psum_tensor("ps", [128, 8, 512], F32))
        self.pb = [Buf("pb%d" % i) for i in range(8)]

    def name(self, n):
        self.uid += 1
        return "%s_%d" % (n, self.uid)

    def tile(self, name, shape, dt):
        return self.scopes[-1].enter_context(self.nc.sbuf_tensor(self.name(name), shape, dt))

    def get_dsem(self, q):
        if self.dsem_free[q]:
            d = self.dsem_free[q].pop()
        else:
            d = DSem(self.root.enter_context(self.nc.semaphore(self.name("d" + q))))
            d.q = q
            self.dsem_all.append(d)
        self.scope_dsems[-1].append(d)
        return d

    def tb(self, name, shape, dt, dma=False):
        t = self.tile(name, shape, dt)
        return TB(t, Buf(name, dma))

    def ring(self, name, n, shape, dt, dma=False):
        return Ring([self.tb("%s%d" % (name, i), shape, dt, dma) for i in range(n)])

    def phase_begin(self):
        st = ExitStack()
        self.scopes.append(st)
        self.scope_dsems.append([])

    def phase_end(self):
        self.barrier()
        st = self.scopes.pop()
        st.close()
        for d in self.scope_dsems.pop():
            self.dsem_free[d.q].append(d)

    def barrier(self):
        sp = self.E["sp"]
        for E in self.E.values():
            assert not E.pending, "pending events at barrier on " + E.name
            if E is not sp and E.cnt > sp.seen.get(E, 0):
                sp.e.wait_ge(E.sem, E.cnt)
        for d in self.dsem_all:
            if d.cnt > sp.seen.get(d, 0):
                sp.e.wait_ge(d.sem, d.cnt)
        self.bar_cnt += 1
        sp.e.sem_inc(self.bar_sem, 1)
        for E in self.E.values():
            if E is not sp:
                E.e.wait_ge(self.bar_sem, self.bar_cnt)
            for Y in self.E.values():
                E.seen[Y] = Y.cnt
            for d in self.dsem_all:
                E.seen[d] = d.cnt

    def _deps(self, reads, writes, skip_key=None):
        deps = []
        for b in reads:
            deps.extend(b.w.values())
        for b in writes:
            for ev in b.w.values():
                if ev.key is not skip_key:
                    deps.append(ev)
            deps.extend(b.r.values())
        return deps

    def _wait(self, E, deps):
        pe = self.E["pe"]
        for ev in deps:
            if ev.key is E and E is pe:
                continue
            assert ev.cnt is not None, "wait on unsignaled event"
            if E.seen.get(ev.key, 0) >= ev.cnt:
                continue
            E.e.wait_ge(ev.key.sem, ev.cnt)
            E.seen[ev.key] = ev.cnt

    def _record(self, ev, reads, writes):
        for b in reads:
            b.r[ev.key] = ev
        for b in writes:
            b.w = {ev.key: ev}
            b.r = {}

    def op(self, eng, fn, reads=(), writes=(), signal=True):
        E = self.E[eng]
        self._wait(E, self._deps(reads, writes))
        inst = fn(E.e)
        if signal:
            E.cnt += 1
            inst.then_inc(E.sem, 1)
            ev = Ev(E, E.cnt)
            for p in E.pending:
                p.cnt = E.cnt
            E.pending = []
        else:
            ev = Ev(E, None)
            E.pending.append(ev)
        self._record(ev, reads, writes)
        return ev

    def mm_group(self, out, pairs, reads, writes, signal=True, first=True, last=True, skip=False):
        E = self.E["pe"]
        self._wait(E, self._deps(reads, writes))
        n = len(pairs)
        inst = None
        for i, (l, r) in enumerate(pairs):
            if skip:
                inst = E.e.matmul(out, l, r, start=(first and i == 0), stop=(last and i == n - 1),
                                  skip_group_check=True)
            else:
                inst = E.e.matmul(out, l, r, start=(first and i == 0), stop=(last and i == n - 1))
        if signal:
            E.cnt += 1
            inst.then_inc(E.sem, 1)
            ev = Ev(E, E.cnt)
            for p in E.pending:
                p.cnt = E.cnt
            E.pending = []
        else:
            ev = Ev(E, None)
            E.pending.append(ev)
        self._record(ev, reads, writes)
        return ev

    def dma(self, q, out, in_, reads=(), writes=(), dsem=None, **kw):
        Q = self.E[q]
        if dsem is None:
            for b in list(writes) + list(reads):
                if b.dma:
                    if b.dsem is None:
                        b.dsem = self.get_dsem(q)
                    dsem = b.dsem
                    break
        assert dsem is not None and dsem.q == q
        self._wait(Q, self._deps(reads, writes, skip_key=dsem))
        inst = Q.e.dma_start(out=out, in_=in_, **kw)
        dsem.cnt += 16
        inst.then_inc(dsem.sem, 16)
        ev = Ev(dsem, dsem.cnt)
        for b in reads:
            b.r[ev.key] = ev
        for b in writes:
            b.w = {ev.key: ev}
            b.r = {}
        return ev

    def wait_all_dma(self, eng, bufs):
        E = self.E[eng]
        for b in bufs:
            self._wait(E, list(b.r.values()) + list(b.w.values()))


def run_pipeline(jobs):
    n = len(jobs)
    if n == 0:
        return
    S = max(len(j) for j in jobs)
    for i in range(n + S - 1):
        for s in range(S):
            ji = i - s
            if 0 <= ji < n and s < len(jobs[ji]) and jobs[ji][s] is not None:
                jobs[ji][s]()


class Prog:
    def __init__(self, layers, first, last_launch):
        self.layers = layers
        self.first = first
        self.final = last_launch
        nc = bass.Bass("TRN2", target_bir_lowering=False)
        self.nc = nc
        self.k = K(nc)
        self.inputs = {}
        self.build()

    def din(self, name, shape):
        t = self.nc.dram_tensor(name, list(shape), F32, kind="ExternalInput")
        self.inputs[name] = tuple(shape)
        return t.ap()

    def build(self):
        nc, k = self.nc, self.k
        Ls = self.layers
        if self.first:
            self.x_in = self.din("x", (SEQ, D))
            self.ctx_in = self.din("ctx", (NCTX, D))
            self.XT = nc.dram_tensor("XT", [D, T], F32).ap()
        else:
            self.XT_in = self.din("xt_in", (D, T))
            self.XT = self.XT_in
        if self.final:
            self.OUT = nc.dram_tensor("out", [SEQ, D], F32, kind="ExternalOutput").ap()
            self.XTw = self.XT if self.first else nc.dram_tensor("XTs", [D, T], F32).ap()
        else:
            self.XTw = nc.dram_tensor("xt_out", [D, T], F32, kind="ExternalOutput").ap()
        self.c_in = self.din("c", (D,))
        self.cctx_in = self.din("c_ctx", (D,))
        self.W = {}
        need = _needed(Ls)
        for nm, shp in WEIGHT_SHAPES.items():
            if nm in need:
                self.W[nm] = self.din(nm, shp)
        for nm, shp in CONST_SHAPES.items():
            if nm in need:
                self.W[nm] = self.din(nm, shp)
        dk = "ExternalOutput" if DEBUG else "Internal"
        self.QT = nc.dram_tensor("QT", [D, T], BF16, kind=dk).ap()
        self.KTd = nc.dram_tensor("KTd", [D, T], BF16, kind=dk).ap()
        self.VH = nc.dram_tensor("VH", [8, T, 128], BF16, kind=dk).ap()
        self.OT = nc.dram_tensor("OT", [D, T], BF16, kind=dk).ap()

        self.ident_f = k.tb("ident_f", [128, 128], F32, dma=True)
        self.ones_f = k.tb("ones_f", [128, 128], F32)
        self.ones_b = k.tb("ones_b", [128, 128], BF16)
        self.bd64 = k.tb("bd64", [128, 128], BF16, dma=True)
        self.r64t = k.tb("r64t", [128, 128], BF16, dma=True)
        self.r128t = k.tb("r128t", [128, 128], BF16, dma=True)
        self.mod = k.tb("mod", [128, DEPTH, 48, 2], F32)
        self.g1t = k.tb("g1t", [128, DEPTH, 8, 2], F32)
        self.g2t = k.tb("g2t", [128, DEPTH, 8, 2], F32)
        k.dma("sp", self.ident_f.t[:], self.W["c_ident"], writes=[self.ident_f.b])
        k.dma("pool", self.bd64.t[:], self.W["c_bd64"], writes=[self.bd64.b])
        k.dma("pool", self.r64t.t[:], self.W["c_r64t"], writes=[self.r64t.b])
        k.dma("pool", self.r128t.t[:], self.W["c_r128t"], writes=[self.r128t.b])
        k.op("dve", lambda e: e.memset(self.ones_f.t[:], 1.0), writes=[self.ones_f.b])
        k.op("dve", lambda e: e.memset(self.ones_b.t[:], 1.0), writes=[self.ones_b.b])

        self.prologue_mod()
        if self.first:
            self.prologue_x()
        for li, L in enumerate(Ls):
            self.xt_r = self.XT if li == 0 else self.XTw
            last = (L == DEPTH - 1)
            kind = L % 3
            if kind in (0, 1):
                self.p1_attn(L, kind)
                self.p2_attn(L, kind, last)
            else:
                self.p1_gla(L)
                self.p2_gla(L, last)
            self.p3(L, kind, last)
        k.barrier()
        k.root.close()

    def prologue_mod(self):
        k, W = self.k, self.W
        k.phase_begin()
        craw = k.tb("craw", [128, 8, 2], F32, dma=True)
        cnd = k.tb("cnd", [128, 8, 2], F32)
        adb = k.tb("adb", [128, DEPTH, 48], F32, dma=True)
        n1g = k.tb("n1g", [128, DEPTH, 8], F32, dma=True)
        n2g = k.tb("n2g", [128, DEPTH, 8], F32, dma=True)
        aw = k.ring("aw", 2, [128, 6144], F32, dma=True)
        tmp = k.tb("modtmp", [128, 8, 2], F32)
        k.dma("sp", craw.t[:, :, 0], self.c_in.rearrange("(c p) -> p c", p=128), writes=[craw.b],
              allow_slow_non_contiguous=True)
        k.dma("sp", craw.t[:, :, 1], self.cctx_in.rearrange("(c p) -> p c", p=128), writes=[craw.b],
              allow_slow_non_contiguous=True)
        k.dma("sp", adb.t[:], W["ada_b"].rearrange("l (c p) -> p l c", p=128), writes=[adb.b],
              allow_slow_non_contiguous=True)
        k.dma("sp", n1g.t[:], W["norm1_g"].rearrange("l (c p) -> p l c", p=128), writes=[n1g.b],
              allow_slow_non_contiguous=True)
        k.dma("sp", n2g.t[:], W["norm2_g"].rearrange("l (c p) -> p l c", p=128), writes=[n2g.b],
              allow_slow_non_contiguous=True)
        k.op("act", lambda e: e.activation(out=cnd.t[:], in_=craw.t[:], func=AF.Silu),
             reads=[craw.b], writes=[cnd.b])
        ps = k.ps
        for L in self.layers:
            for kc in range(KC):
                a = aw.next()
                k.dma("sp", a.t[:], W["ada_w"][L, kc * 128:(kc + 1) * 128, :], writes=[a.b])
                for f in range(48):
                    k.mm_group(ps[:, 0, 2 * f:2 * f + 2], [(a.t[:, f * 128:(f + 1) * 128], cnd.t[:, kc, :])],
                               reads=[a.b, cnd.b], writes=[k.pb[0]], signal=(f == 47),
                               first=(kc == 0 and f == 0), last=(kc == KC - 1), skip=True)
            for v in range(2):
                k.op("dve", lambda e, v=v, L=L: e.tensor_tensor(
                    out=self.mod.t[:, L, :, v], in0=ps[:, 0, 0:96].rearrange("p (f v) -> p f v", v=2)[:, :, v],
                    in1=adb.t[:, L, :], op=ALU.add),
                    reads=[k.pb[0], adb.b], writes=[self.mod.b])
            for (gt, ng, off) in ((self.g1t, n1g, 8), (self.g2t, n2g, 32)):
                for v in range(2):
                    k.op("dve", lambda e, v=v, L=L, off=off: e.tensor_scalar(
                        out=tmp.t[:, :, v], in0=self.mod.t[:, L, off:off + 8, v], scalar1=1.0, scalar2=None,
                        op0=ALU.add), reads=[self.mod.b], writes=[tmp.b])
                    k.op("dve", lambda e, v=v, L=L, gt=gt, ng=ng: e.tensor_tensor(
                        out=gt.t[:, L, :, v], in0=tmp.t[:, :, v], in1=ng.t[:, L, :], op=ALU.mult),
                        reads=[tmp.b, ng.b], writes=[gt.b])
        k.phase_end()

    def prologue_x(self):
        k = self.k
        k.phase_begin()
        xin = k.ring("xin", 8, [128, D], F32, dma=True)
        xst = k.ring("xst", 2, [128, KC, 512], F32, dma=True)
        ps = k.ps
        bank = [0]
        blocks = [(0, NCTX)] + [(NCTX + 512 * i, 512) for i in range(SEQ // 512)]
        cnt = [0]
        for (t0, n) in blocks:
            nt = n // 128
            tiles = []
            for i in range(nt):
                xi = xin.next()
                src = self.ctx_in[i * 128:(i + 1) * 128, :] if t0 == 0 else \
                    self.x_in[t0 - NCTX + i * 128: t0 - NCTX + (i + 1) * 128, :]
                k.dma("sp", xi.t[:], src, writes=[xi.b])
                tiles.append(xi)
            st = xst.next()
            for c in range(KC):
                b = bank[0] % 8
                bank[0] += 1
                E = k.E["pe"]
                k._wait(E, k._deps([t.b for t in tiles] + [self.ident_f.b], [k.pb[b]]))
                inst = None
                for i, xi in enumerate(tiles):
                    inst = E.e.transpose(ps[:, b, i * 128:(i + 1) * 128], xi.t[:, c * 128:(c + 1) * 128],
                                         self.ident_f.t[:])
                E.cnt += 1
                inst.then_inc(E.sem, 1)
                ev = Ev(E, E.cnt)
                k._record(ev, [t.b for t in tiles], [k.pb[b]])
                eng = "act" if cnt[0] % 2 == 0 else "dve"
                cnt[0] += 1
                if eng == "act":
                    k.op("act", lambda e, b=b, c=c, st=st, n=n: e.activation(
                        out=st.t[:, c, 0:n], in_=ps[:, b, 0:n], func=AF.Copy), reads=[k.pb[b]], writes=[st.b])
                else:
                    k.op("dve", lambda e, b=b, c=c, st=st, n=n: e.tensor_copy(
                        out=st.t[:, c, 0:n], in_=ps[:, b, 0:n]), reads=[k.pb[b]], writes=[st.b])
            k.dma("sp", self.XT.rearrange("(c p) t -> p c t", p=128)[:, :, t0:t0 + n], st.t[:, :, 0:n],
                  reads=[st.b])
        k.phase_end()

    def norm_part1(self, env, blk, src_ap, xt, sq=None):
        k = self.k
        t0, n, v = blk
        if src_ap is not None:
            k.dma("sp", xt.t[:, :, 0:n], src_ap.rearrange("(c p) t -> p c t", p=128)[:, :, t0:t0 + n],
                  writes=[xt.b])
        if sq is None:
            sq = env["nsq"].next()
        for c in range(KC):
            k.op("act", lambda e, c=c: e.activation(out=sq.t[:, c, 0:n], in_=xt.t[:, c, 0:n], func=AF.Square),
                 reads=[xt.b], writes=[sq.b])
        return sq

    def norm_part2(self, env, blk, xt, sq, L, gt, sh_off, hT, bank):
        k = self.k
        t0, n, v = blk
        ps = k.ps
        k.mm_group(ps[:, bank, 0:n], [(self.ones_b.t[:], sq.t[:, c, 0:n]) for c in range(KC)],
                   reads=[sq.b, self.ones_b.b], writes=[k.pb[bank]])
        ln = env["nln"].next()
        rs = env["nrs"].next()
        k.op("act", lambda e: e.activation(out=ln.t[:, 0:n], in_=ps[:, bank, 0:n], func=AF.Ln,
                                           scale=1.0 / D, bias=EPS), reads=[k.pb[bank]], writes=[ln.b])
        k.op("act", lambda e: e.activation(out=rs.t[:, 0:n], in_=ln.t[:, 0:n], func=AF.Exp, scale=-0.5),
             reads=[ln.b], writes=[rs.b])
        for c in range(KC):
            tm = env["ntm"].next()
            k.op("dve", lambda e, c=c, tm=tm: e.tensor_tensor(out=tm.t[:, 0:n], in0=xt.t[:, c, 0:n],
                                                             in1=rs.t[:, 0:n], op=ALU.mult),
                 reads=[xt.b, rs.b], writes=[tm.b])
            k.op("act", lambda e, c=c, tm=tm: e.activation(
                out=hT.t[:, c, 0:n], in_=tm.t[:, 0:n], func=AF.Identity,
                scale=gt.t[:, L, c, v:v + 1], bias=self.mod.t[:, L, sh_off + c, v:v + 1]),
                reads=[tm.b, gt.b, self.mod.b], writes=[hT.b])

    def norm_env(self, n=512):
        k = self.k
        if n == 512:
            return {
                "nsq": k.ring("nsq", 2, [128, KC, n], BF16),
                "nln": k.ring("nln", 2, [128, n], F32),
                "nrs": k.ring("nrs", 2, [128, n], F32),
                "ntm": k.ring("ntm", 3, [128, n], F32),
            }
        return {
            "nln": k.ring("nln", 1, [128, n], F32),
            "nrs": k.ring("nrs", 1, [128, n], F32),
            "ntm": k.ring("ntm", 2, [128, n], F32),
        }

    def p1_attn(self, L, kind):
        k, W = self.k, self.W
        j = L // 3
        ps = k.ps
        if kind == 0:
            w_in = W["diff_w_in"][j]
            ncols, nq, nk, vcols, dh = 3072, 8, 8, 1024, 64
            qn_src, kn_src = W["diff_q_norm"][j], W["diff_k_norm"][j]
            bd, rt = self.bd64, self.r64t
            cos_d, sin_d = W["c_cos64"], W["c_sin64"]
        else:
            w_in = W["gqa_w_in"][j]
            ncols, nq, nk, vcols, dh = 1536, 8, 2, 256, 128
            qn_src, kn_src = W["gqa_q_norm"][j], W["gqa_k_norm"][j]
            bd, rt = self.ones_b, self.r128t
            cos_d, sin_d = W["c_cos128"], W["c_sin128"]
        k.phase_begin()
        win = k.tb("win", [128, KC, ncols], BF16, dma=True)
        for c in range(KC):
            k.dma("pool", win.t[:, c, :], w_in[c * 128:(c + 1) * 128, :], writes=[win.b])
        ncol = k.tb("ncol", [128, 2], F32, dma=True)
        for i, src in enumerate((qn_src, kn_src)):
            for r in range(128 // dh):
                k.dma("sp", ncol.t[r * dh:(r + 1) * dh, i:i + 1], src.rearrange("(p o) -> p o", o=1),
                      writes=[ncol.b], allow_slow_non_contiguous=True)
        env = self.norm_env(512)
        xtr = k.ring("xt", 2, [128, KC, 512], F32, dma=True)
        hTr = k.ring("hT", 2, [128, KC, 512], BF16)
        cosr = k.ring("cos", 2, [128, 512], F32, dma=True)
        sinr = k.ring("sin", 2, [128, 512], F32, dma=True)
        sqq = k.ring("sqq", 3, [128, 512], BF16)
        lnq = k.ring("lnq", 2, [128, 512], F32)
        rsq = k.ring("rsq", 2, [128, 512], F32)
        qnr = k.ring("qn", 3, [128, 512], BF16)
        t1r = k.ring("t1", 2, [128, 512], F32)
        t2r = k.ring("t2", 2, [128, 512], F32)
        qor = k.ring("qo", 3, [128, 512], BF16, dma=True)
        vst = k.ring("vst", 3, [128, 512], BF16, dma=True)
        pj = Ring([1, 2, 3])
        pm = Ring([4, 5])
        pr = Ring([6, 7])
        blocks = [(0, NCTX, 1)] + [(NCTX + 512 * i, 512, 0) for i in range(SEQ // 512)]
        state = {}
        jobs = []
        cp = [0]

        def mk_block_jobs(bi):
            blk = blocks[bi]
            t0, n, v = blk

            def part1():
                xt = xtr.next()
                sq = self.norm_part1(env, blk, self.xt_r, xt)
                cs, sn = cosr.next(), sinr.next()
                k.dma("sp", cs.t[:, 0:n], cos_d[:, t0:t0 + n], writes=[cs.b])
                k.dma("sp", sn.t[:, 0:n], sin_d[:, t0:t0 + n], writes=[sn.b])
                state[bi] = {"xt": xt, "sq": sq, "cos": cs, "sin": sn}

            def part2():
                st = state[bi]
                hT = hTr.next()
                st["hT"] = hT
                self.norm_part2(env, blk, st["xt"], st["sq"], L, self.g1t, 0, hT, 0)

            out = [[part1], [part2]]
            for ci in range(nq + nk):
                isq = ci < nq
                col0 = ci * 128
                d = {}

                def A(d=d, col0=col0):
                    st = state[bi]
                    d["b"] = pj.next()
                    k.mm_group(ps[:, d["b"], 0:n],
                               [(win.t[:, c, col0:col0 + 128], st["hT"].t[:, c, 0:n]) for c in range(KC)],
                               reads=[win.b, st["hT"].b], writes=[k.pb[d["b"]]])

                def B(d=d):
                    d["sq"] = sqq.next()
                    k.op("act", lambda e: e.activation(out=d["sq"].t[:, 0:n], in_=ps[:, d["b"], 0:n],
                                                       func=AF.Square), reads=[k.pb[d["b"]]], writes=[d["sq"].b])
                    d["m"] = pm.next()
                    k.mm_group(ps[:, d["m"], 0:n], [(bd.t[:], d["sq"].t[:, 0:n])],
                               reads=[bd.b, d["sq"].b], writes=[k.pb[d["m"]]])

                def C(d=d, isq=isq):
                    ln, rs, qn = lnq.next(), rsq.next(), qnr.next()
                    d["qn"] = qn
                    k.op("act", lambda e: e.activation(out=ln.t[:, 0:n], in_=ps[:, d["m"], 0:n], func=AF.Ln,
                                                       scale=1.0 / dh, bias=EPS), reads=[k.pb[d["m"]]], writes=[ln.b])
                    k.op("act", lambda e: e.activation(out=rs.t[:, 0:n], in_=ln.t[:, 0:n], func=AF.Exp, scale=-0.5),
                         reads=[ln.b], writes=[rs.b])
                    ci_ = 0 if isq else 1
                    k.op("dve", lambda e: e.scalar_tensor_tensor(
                        out=qn.t[:, 0:n], in0=ps[:, d["b"], 0:n], scalar=ncol.t[:, ci_:ci_ + 1], in1=rs.t[:, 0:n],
                        op0=ALU.mult, op1=ALU.mult), reads=[k.pb[d["b"]], ncol.b, rs.b], writes=[qn.b])
                    d["r"] = pr.next()
                    k.mm_group(ps[:, d["r"], 0:n], [(rt.t[:], qn.t[:, 0:n])], reads=[rt.b, qn.b],
                               writes=[k.pb[d["r"]]])

                def Dd(d=d, isq=isq, ci=ci):
                    st = state[bi]
                    t1, t2, qo = t1r.next(), t2r.next(), qor.next()
                    k.op("dve", lambda e: e.tensor_tensor(out=t1.t[:, 0:n], in0=d["qn"].t[:, 0:n],
                                                          in1=st["cos"].t[:, 0:n], op=ALU.mult),
                         reads=[d["qn"].b, st["cos"].b], writes=[t1.b])
                    k.op("dve", lambda e: e.tensor_tensor(out=t2.t[:, 0:n], in0=ps[:, d["r"], 0:n],
                                                          in1=st["sin"].t[:, 0:n], op=ALU.mult),
                         reads=[k.pb[d["r"]], st["sin"].b], writes=[t2.b])
                    k.op("pool", lambda e: e.tensor_tensor(out=qo.t[:, 0:n], in0=t1.t[:, 0:n], in1=t2.t[:, 0:n],
                                                           op=ALU.add), reads=[t1.b, t2.b], writes=[qo.b])
                    if isq:
                        dst = self.QT[ci * 128:(ci + 1) * 128, t0:t0 + n]
                    else:
                        dst = self.KTd[(ci - nq) * 128:(ci - nq + 1) * 128, t0:t0 + n]
                    k.dma("sp", dst, qo.t[:, 0:n], reads=[qo.b])

                out.append([A, B, C, Dd])
            for ti in range(n // 128):
                for cg in range(vcols // min(vcols, 512)):
                    cw = min(vcols, 512)
                    d = {}

                    def A(d=d, ti=ti, cg=cg, cw=cw):
                        st = state[bi]
                        d["b"] = pj.next()
                        c0 = nq * 128 + nk * 128 + cg * cw
                        k.mm_group(ps[:, d["b"], 0:cw],
                                   [(st["hT"].t[:, c, ti * 128:(ti + 1) * 128], win.t[:, c, c0:c0 + cw])
                                    for c in range(KC)],
                                   reads=[win.b, st["hT"].b], writes=[k.pb[d["b"]]])

                    def B(d=d, ti=ti, cg=cg, cw=cw):
                        vs = vst.next()
                        cp[0] += 1
                        if cp[0] % 2:
                            k.op("act", lambda e: e.activation(out=vs.t[:, 0:cw], in_=ps[:, d["b"], 0:cw],
                                                               func=AF.Copy), reads=[k.pb[d["b"]]], writes=[vs.b])
                        else:
                            k.op("dve", lambda e: e.tensor_copy(out=vs.t[:, 0:cw], in_=ps[:, d["b"], 0:cw]),
                                 reads=[k.pb[d["b"]]], writes=[vs.b])
                        h0 = cg * (cw // 128)
                        tt = t0 + ti * 128
                        dst = self.VH[h0:h0 + cw // 128, tt:tt + 128, :].rearrange("h t d -> t h d")
                        k.dma("sp", dst, vs.t[:, 0:cw].rearrange("p (h d) -> p h d", d=128), reads=[vs.b])

                    out.append([A, B])
            return out

        allj = []
        bj = [mk_block_jobs(bi) for bi in range(len(blocks))]
        allj.extend(bj[0][0:2])
        for bi in range(len(blocks)):
            body = bj[bi][2:]
            nxt = bj[bi + 1][0:2] if bi + 1 < len(blocks) else []
            if nxt:
                body = body[0:2] + [nxt[0]] + body[2:8] + [nxt[1]] + body[8:]
            allj.extend(body)
        run_pipeline(allj)
        k.phase_end()

    def p2_attn(self, L, kind, last):
        k, W = self.k, self.W
        j = L // 3
        ps = k.ps
        k.phase_begin()
        KTr = k.ring("KT", 2, [128, T], BF16, dma=True)
        VTr = k.ring("VT", 2, [128, NT, 128], BF16, dma=True)
        QTr = k.ring("QTt", 2, [128, T], BF16, dma=True)
        qblocks = ([] if last else [(0, NCTX, NCTX // 128)]) + [(NCTX + 512 * i, 512, NT) for i in range(SEQ // 512)]
        if kind == 0:
            dh = 64
            scale = dh ** -0.5
            PTr = k.ring("PT", 6, [128, 1024], BF16)
            accr = k.ring("acc", 2, [128, 1024], F32)
            tpr = k.ring("tp", 2, [128, 1024], BF16)
            r1r = k.ring("r1", 2, [128, 512], F32)
            r2r = k.ring("r2", 2, [128, 512], F32)
            a1r = k.ring("a1", 2, [128, 512], F32)
            a2r = k.ring("a2", 2, [128, 512], F32)
            ost = k.ring("ost", 3, [128, 512], BF16, dma=True)
            lambda_init = 0.8 - 0.6 * math.exp(-0.3 * L)
            lv = k.tb("lv", [64, 4], F32, dma=True)
            for i, nm in enumerate(("diff_lambda_q1", "diff_lambda_k1", "diff_lambda_q2", "diff_lambda_k2")):
                k.dma("sp", lv.t[:, i:i + 1], W[nm][j].rearrange("(p o) -> p o", o=1), writes=[lv.b],
                      allow_slow_non_contiguous=True)
            lp = k.tb("lp", [64, 2], F32)
            for i in range(2):
                k.op("dve", lambda e, i=i: e.tensor_tensor(out=lp.t[:, i:i + 1], in0=lv.t[:, 2 * i:2 * i + 1],
                                                           in1=lv.t[:, 2 * i + 1:2 * i + 2], op=ALU.mult),
                     reads=[lv.b], writes=[lp.b])
            k.mm_group(ps[:, 0, 0:2], [(self.ones_f.t[0:64, :], lp.t[:, 0:2])], reads=[self.ones_f.b, lp.b],
                       writes=[k.pb[0]])
            le = k.tb("le", [128, 2], F32)
            neglam = k.tb("neglam", [128, 1], F32)
            k.op("act", lambda e: e.activation(out=le.t[:], in_=ps[:, 0, 0:2], func=AF.Exp), reads=[k.pb[0]],
                 writes=[le.b])
            k.op("dve", lambda e: e.tensor_tensor(out=neglam.t[:], in0=le.t[:, 1:2], in1=le.t[:, 0:1],
                                                  op=ALU.subtract), reads=[le.b], writes=[neglam.b])
            k.op("dve", lambda e: e.tensor_scalar(out=neglam.t[:], in0=neglam.t[:], scalar1=-lambda_init,
                                                  scalar2=None, op0=ALU.add), reads=[neglam.b], writes=[neglam.b])
            sset = Ring([(0, 1), (2, 3), (6, 7)])
            heads = list(range(8))
            hd = {}

            def load_head(h):
                kt, vt, qt = KTr.next(), VTr.next(), QTr.next()
                k.dma("sp", kt.t[:], self.KTd[h * 128:(h + 1) * 128, :], writes=[kt.b])
                k.dma("sp", qt.t[:], self.QT[h * 128:(h + 1) * 128, :], writes=[qt.b])
                vsrc = self.VH[h].rearrange("(n p) d -> p n d", p=128)
                for a in range(0, NT, 22):
                    k.dma("sp", vt.t[:, a:min(a + 22, NT), :], vsrc[:, a:min(a + 22, NT), :], writes=[vt.b])
                hd[h] = (kt, vt, qt)

            units = []
            for h in heads:
                for (q0, n, nkt) in qblocks:
                    for kt in range(nkt):
                        units.append((h, q0, n, nkt, kt))

            def S(u, d):
                h, q0, n, nkt, kti = u
                kt, vt, qt = hd[h]
                s = sset.next()
                d["s"] = s
                E = k.E["pe"]
                k._wait(E, k._deps([kt.b, qt.b], [k.pb[s[0]], k.pb[s[1]]]))
                E.e.matmul(ps[:, s[0], 0:n], kt.t[0:64, kti * 128:(kti + 1) * 128], qt.t[0:64, q0:q0 + n],
                           start=True, stop=True)
                inst = E.e.matmul(ps[:, s[1], 0:n], kt.t[64:128, kti * 128:(kti + 1) * 128],
                                  qt.t[64:128, q0:q0 + n], start=True, stop=True)
                E.cnt += 1
                inst.then_inc(E.sem, 1)
                k._record(Ev(E, E.cnt), [kt.b, qt.b], [k.pb[s[0]], k.pb[s[1]]])

            def X(u, d):
                h, q0, n, nkt, kti = u
                s = d["s"]
                pt = PTr.next()
                d["pt"] = pt
                k.op("act", lambda e: e.activation(
                    out=pt.t[:, 0:2 * n].rearrange("p (a n) -> p a n", a=2), in_=ps[:, s[0]:s[0] + 2, 0:n],
                    func=AF.Exp, scale=scale), reads=[k.pb[s[0]], k.pb[s[1]]], writes=[pt.b])

            qst = {}

            def PV(u, d):
                h, q0, n, nkt, kti = u
                kt, vt, qt = hd[h]
                pt = d["pt"]
                E = k.E["pe"]
                k._wait(E, k._deps([vt.b, pt.b], [k.pb[4], k.pb[5]]))
                st, sp_ = (kti == 0), (kti == nkt - 1)
                E.e.matmul(ps[:, 4, 0:n], vt.t[:, kti, :], pt.t[:, 0:n], start=st, stop=sp_)
                inst = E.e.matmul(ps[:, 5, 0:n], vt.t[:, kti, :], pt.t[:, n:2 * n], start=st, stop=sp_)
                E.cnt += 1
                inst.then_inc(E.sem, 1)
                k._record(Ev(E, E.cnt), [vt.b, pt.b], [k.pb[4], k.pb[5]])
                if st:
                    qst["acc"] = accr.next()
                acc = qst["acc"]
                if kti % 2 == 0:
                    qst["prev"] = pt
                else:
                    pa = qst["prev"]
                    if kti == 1:
                        k.op("dve", lambda e: e.tensor_tensor(out=acc.t[:, 0:2 * n], in0=pa.t[:, 0:2 * n],
                                                              in1=pt.t[:, 0:2 * n], op=ALU.add),
                             reads=[pa.b, pt.b], writes=[acc.b])
                    else:
                        tp = tpr.next()
                        k.op("dve", lambda e: e.tensor_tensor(out=tp.t[:, 0:2 * n], in0=pa.t[:, 0:2 * n],
                                                              in1=pt.t[:, 0:2 * n], op=ALU.add),
                             reads=[pa.b, pt.b], writes=[tp.b])
                        k.op("dve", lambda e: e.tensor_tensor(out=acc.t[:, 0:2 * n], in0=acc.t[:, 0:2 * n],
                                                              in1=tp.t[:, 0:2 * n], op=ALU.add),
                             reads=[acc.b, tp.b], writes=[acc.b])
                if not sp_:
                    return None
                a1, a2 = a1r.next(), a2r.next()
                k.op("dve", lambda e: e.tensor_copy(out=a1.t[:, 0:n], in_=ps[:, 4, 0:n]), reads=[k.pb[4]],
                     writes=[a1.b])
                k.op("dve", lambda e: e.tensor_copy(out=a2.t[:, 0:n], in_=ps[:, 5, 0:n]), reads=[k.pb[5]],
                     writes=[a2.b])

                def partB():
                    sb = sset.next()
                    k.mm_group(ps[:, sb[0], 0:n], [(self.ones_f.t[:], acc.t[:, 0:n])], reads=[self.ones_f.b, acc.b],
                               writes=[k.pb[sb[0]]])
                    k.mm_group(ps[:, sb[1], 0:n], [(self.ones_f.t[:], acc.t[:, n:2 * n])],
                               reads=[self.ones_f.b, acc.b], writes=[k.pb[sb[1]]])
                    r1, r2, o = r1r.next(), r2r.next(), ost.next()
                    for (rr, bb) in ((r1, sb[0]), (r2, sb[1])):
                        k.op("act", lambda e, rr=rr, bb=bb: e.activation(out=rr.t[:, 0:n], in_=ps[:, bb, 0:n],
                                                                         func=AF.Ln), reads=[k.pb[bb]], writes=[rr.b])
                        k.op("act", lambda e, rr=rr: e.activation(out=rr.t[:, 0:n], in_=rr.t[:, 0:n], func=AF.Exp,
                                                                  scale=-1.0), reads=[rr.b], writes=[rr.b])
                    k.op("dve", lambda e: e.tensor_tensor(out=a1.t[:, 0:n], in0=a1.t[:, 0:n], in1=r1.t[:, 0:n],
                                                          op=ALU.mult), reads=[a1.b, r1.b], writes=[a1.b])
                    k.op("dve", lambda e: e.tensor_tensor(out=a2.t[:, 0:n], in0=a2.t[:, 0:n], in1=r2.t[:, 0:n],
                                                          op=ALU.mult), reads=[a2.b, r2.b], writes=[a2.b])
                    k.op("dve", lambda e: e.scalar_tensor_tensor(
                        out=o.t[:, 0:n], in0=a2.t[:, 0:n], scalar=neglam.t[:, 0:1], in1=a1.t[:, 0:n],
                        op0=ALU.mult, op1=ALU.add), reads=[a2.b, neglam.b, a1.b], writes=[o.b])
                    k.dma("sp", self.OT[h * 128:(h + 1) * 128, q0:q0 + n], o.t[:, 0:n], reads=[o.b])
                return partB
        else:
            dh = 128
            scale = dh ** -0.5
            PTr = k.ring("PT", 6, [128, 512], BF16)
            accr = k.ring("acc", 2, [128, 512], F32)
            tpr = k.ring("tp", 2, [128, 512], BF16)
            a1r = k.ring("a1", 2, [128, 512], F32)
            r1r = k.ring("r1", 2, [128, 512], F32)
            ost = k.ring("ost", 3, [128, 512], BF16, dma=True)
            sset = Ring([0, 1, 2, 3, 5])
            hd = {}
            kv = {}

            def load_head(h):
                g = h // 4
                if h % 4 == 0:
                    kt, vt = KTr.next(), VTr.next()
                    k.dma("sp", kt.t[:], self.KTd[g * 128:(g + 1) * 128, :], writes=[kt.b])
                    vsrc = self.VH[g].rearrange("(n p) d -> p n d", p=128)
                    for a in range(0, NT, 22):
                        k.dma("sp", vt.t[:, a:min(a + 22, NT), :], vsrc[:, a:min(a + 22, NT), :], writes=[vt.b])
                    kv[g] = (kt, vt)
                qt = QTr.next()
                k.dma("sp", qt.t[:], self.QT[h * 128:(h + 1) * 128, :], writes=[qt.b])
                hd[h] = (kv[g][0], kv[g][1], qt)

            heads = list(range(8))
            units = []
            for h in heads:
                for (q0, n, nkt) in qblocks:
                    for kt in range(nkt):
                        units.append((h, q0, n, nkt, kt))

            def S(u, d):
                h, q0, n, nkt, kti = u
                kt, vt, qt = hd[h]
                s = sset.next()
                d["s"] = s
                k.mm_group(ps[:, s, 0:n], [(kt.t[:, kti * 128:(kti + 1) * 128], qt.t[:, q0:q0 + n])],
                           reads=[kt.b, qt.b], writes=[k.pb[s]])

            def X(u, d):
                h, q0, n, nkt, kti = u
                s = d["s"]
                pt = PTr.next()
                d["pt"] = pt
                k.op("act", lambda e: e.activation(out=pt.t[:, 0:n], in_=ps[:, s, 0:n], func=AF.Exp, scale=scale),
                     reads=[k.pb[s]], writes=[pt.b])

            qst = {}

            def PV(u, d):
                h, q0, n, nkt, kti = u
                kt, vt, qt = hd[h]
                pt = d["pt"]
                st, sp_ = (kti == 0), (kti == nkt - 1)
                k.mm_group(ps[:, 4, 0:n], [(vt.t[:, kti, :], pt.t[:, 0:n])], reads=[vt.b, pt.b], writes=[k.pb[4]],
                           first=st, last=sp_)
                if st:
                    qst["acc"] = accr.next()
                acc = qst["acc"]
                if kti % 2 == 0:
                    qst["prev"] = pt
                else:
                    pa = qst["prev"]
                    if kti == 1:
                        k.op("dve", lambda e: e.tensor_tensor(out=acc.t[:, 0:n], in0=pa.t[:, 0:n], in1=pt.t[:, 0:n],
                                                              op=ALU.add), reads=[pa.b, pt.b], writes=[acc.b])
                    else:
                        tp = tpr.next()
                        k.op("dve", lambda e: e.tensor_tensor(out=tp.t[:, 0:n], in0=pa.t[:, 0:n], in1=pt.t[:, 0:n],
                                                              op=ALU.add), reads=[pa.b, pt.b], writes=[tp.b])
                        k.op("dve", lambda e: e.tensor_tensor(out=acc.t[:, 0:n], in0=acc.t[:, 0:n], in1=tp.t[:, 0:n],
                                                              op=ALU.add), reads=[acc.b, tp.b], writes=[acc.b])
                if not sp_:
                    return None
                a1 = a1r.next()
                k.op("dve", lambda e: e.tensor_copy(out=a1.t[:, 0:n], in_=ps[:, 4, 0:n]), reads=[k.pb[4]],
                     writes=[a1.b])

                def partB():
                    k.mm_group(ps[:, 6, 0:n], [(self.ones_f.t[:], acc.t[:, 0:n])], reads=[self.ones_f.b, acc.b],
                               writes=[k.pb[6]])
                    r1, o = r1r.next(), ost.next()
                    k.op("act", lambda e: e.activation(out=r1.t[:, 0:n], in_=ps[:, 6, 0:n], func=AF.Ln),
                         reads=[k.pb[6]], writes=[r1.b])
                    k.op("act", lambda e: e.activation(out=r1.t[:, 0:n], in_=r1.t[:, 0:n], func=AF.Exp, scale=-1.0),
                         reads=[r1.b], writes=[r1.b])
                    k.op("dve", lambda e: e.tensor_tensor(out=o.t[:, 0:n], in0=a1.t[:, 0:n], in1=r1.t[:, 0:n],
                                                          op=ALU.mult), reads=[a1.b, r1.b], writes=[o.b])
                    k.dma("sp", self.OT[h * 128:(h + 1) * 128, q0:q0 + n], o.t[:, 0:n], reads=[o.b])
                return partB

        load_head(heads[0])
        load_head(heads[1])
        ds = [dict() for _ in units]
        pend = []
        for i, u in enumerate(units):
            S(u, ds[i])
            X(u, ds[i])
            if i >= 1:
                pb_ = PV(units[i - 1], ds[i - 1])
                ds[i - 1] = None
                for p in pend:
                    p[0] -= 1
                while pend and pend[0][0] <= 0:
                    pend.pop(0)[1]()
                if pb_ is not None:
                    pend.append([2, pb_])
                if units[i - 1][0] != u[0] and u[0] + 1 < 8:
                    load_head(u[0] + 1)
        pb_ = PV(units[-1], ds[-1])
        while pend:
            pend.pop(0)[1]()
        if pb_ is not None:
            pb_()
        k.phase_end()

    def p3(self, L, kind, last):
        k, W = self.k, self.W
        j = L // 3
        ps = k.ps
        NB = 256
        k.phase_begin()
        w_out_src = {0: "diff_w_out", 1: "gqa_w_out", 2: "gla_w_out"}[kind]
        wo = k.tb("wo", [128, KC, D], BF16, dma=True)
        wgu = k.tb("wgu", [128, KC, 2 * FH], BF16, dma=True)
        wd = k.tb("wd", [128, FC, D], BF16, dma=True)
        for c in range(KC):
            k.dma("pool", wo.t[:, c, :], W[w_out_src][j][c * 128:(c + 1) * 128, :], writes=[wo.b])
        for c in range(KC):
            k.dma("pool", wgu.t[:, c, :], W["ffn_w_gu"][L][c * 128:(c + 1) * 128, :], writes=[wgu.b])
        for c in range(FC):
            k.dma("pool", wd.t[:, c, :], W["ffn_w_down"][L][c * 128:(c + 1) * 128, :], writes=[wd.b])
        env = self.norm_env(NB)
        xtr = k.ring("xt", 2, [128, KC, NB], F32, dma=True)
        obr = k.ring("ob", 1, [128, KC, NB], BF16, dma=True) if kind != 2 else None
        if kind == 2:
            gtl = {
                "of": k.ring("gof", 2, [128, 2, NB], BF16, dma=True),
                "ob": k.ring("gob", 2, [128, 2, NB], BF16, dma=True),
                "rt": k.ring("grt", 2, [128, 2, NB], BF16, dma=True),
                "s": k.ring("gs", 2, [128, NB], F32),
                "sq": k.ring("gsq", 2, [128, NB], BF16),
                "ln": k.ring("gln", 1, [128, NB], F32),
                "rs": k.ring("grs", 1, [128, NB], F32),
                "gain": k.tb("ggain", [128, 2], F32, dma=True),
            }
            k.dma("sp", gtl["gain"].t[:], W["gla_out_norm"][0].rearrange("(e p) -> p e", p=128),
                  writes=[gtl["gain"].b], allow_slow_non_contiguous=True)
        hxr = k.ring("hx", 2, [128, KC, NB], BF16)
        actr = k.ring("actT", 1, [128, FC, NB], BF16)
        sgr = k.ring("sg", 4, [128, NB], F32)
        if kind == 0:
            lambda_init = 0.8 - 0.6 * math.exp(-0.3 * L)
            sln = k.tb("sln", [128, 1], F32, dma=True)
            k.dma("sp", sln.t[:, 0:1], W["diff_subln"][j].rearrange("(p o) -> p o", o=1), writes=[sln.b],
                  allow_slow_non_contiguous=True)
            k.op("dve", lambda e: e.tensor_scalar(out=sln.t[:], in0=sln.t[:], scalar1=1.0 - lambda_init,
                                                  scalar2=None, op0=ALU.mult), reads=[sln.b], writes=[sln.b])
            osq = k.ring("osq", 1, [128, NB], BF16)
            oln = k.ring("oln", 1, [128, NB], F32)
            ors = k.ring("ors", 1, [128, NB], F32)
        if last:
            ost = k.ring("fin", 1, [128, D], F32, dma=True)
        blocks = ([] if last else [(0, NCTX, 1)]) + [(NCTX + NB * i, NB, 0) for i in range(SEQ // NB)]
        pj = Ring([1, 2, 3, 4, 5, 6, 7])
        state = {}

        def load(bi):
            t0, n, v = blocks[bi]
            xt = xtr.next()
            k.dma("sp", xt.t[:, :, 0:n], self.xt_r.rearrange("(c p) t -> p c t", p=128)[:, :, t0:t0 + n],
                  writes=[xt.b])
            state[bi] = {"xt": xt}

        obst = {}

        def load_ob(bi):
            t0, n, v = blocks[bi]
            ob = obr.next() if obr is not None else None
            if kind in (0, 1):
                k.dma("sp", ob.t[:, :, 0:n], self.OT.rearrange("(c p) t -> p c t", p=128)[:, :, t0:t0 + n],
                      writes=[ob.b])
            obst[bi] = ob

        def pre_steps(bi):
            blk = blocks[bi]
            t0, n, v = blk
            st = state[bi]
            xt = st["xt"]
            steps = []

            def begin():
                st["hx"] = hxr.next()
                st["on"] = obst[bi] if kind == 1 else st["hx"]
            steps.append(begin)
            if kind == 0:
                for c in range(KC):
                    def chain(c=c):
                        ob, on = obst[bi], st["on"]
                        sq, ln, rs = osq.next(), oln.next(), ors.next()
                        b = pj.next()
                        k.op("pool", lambda e: e.tensor_tensor(out=sq.t[:, 0:n], in0=ob.t[:, c, 0:n],
                                                               in1=ob.t[:, c, 0:n], op=ALU.mult),
                             reads=[ob.b], writes=[sq.b])
                        k.mm_group(ps[:, b, 0:n], [(self.ones_b.t[:], sq.t[:, 0:n])], reads=[self.ones_b.b, sq.b],
                                   writes=[k.pb[b]])
                        k.op("act", lambda e: e.activation(out=ln.t[:, 0:n], in_=ps[:, b, 0:n], func=AF.Ln,
                                                           scale=1.0 / 128, bias=EPS), reads=[k.pb[b]], writes=[ln.b])
                        k.op("act", lambda e: e.activation(out=rs.t[:, 0:n], in_=ln.t[:, 0:n], func=AF.Exp,
                                                           scale=-0.5), reads=[ln.b], writes=[rs.b])
                        k.op("dve", lambda e: e.scalar_tensor_tensor(
                            out=on.t[:, c, 0:n], in0=ob.t[:, c, 0:n], scalar=sln.t[:, 0:1], in1=rs.t[:, 0:n],
                            op0=ALU.mult, op1=ALU.mult), reads=[ob.b, sln.b, rs.b], writes=[on.b])
                    steps.append(chain)
            elif kind == 2:
                for hh in range(4):
                    steps.append(lambda hh=hh: self.gla_mixer_out(L, blk, st["on"], env, pj, gtl, hh))
            if kind == 0 and bi + 1 < len(blocks):
                steps.append(lambda: load_ob(bi + 1))
            for c in range(KC):
                def oproj(c=c):
                    on = st["on"]
                    b = pj.next()
                    k.mm_group(ps[:, b, 0:n], [(wo.t[:, kc, c * 128:(c + 1) * 128], on.t[:, kc, 0:n])
                                               for kc in range(KC)], reads=[wo.b, on.b], writes=[k.pb[b]])
                    k.op("dve", lambda e: e.scalar_tensor_tensor(
                        out=xt.t[:, c, 0:n], in0=ps[:, b, 0:n], scalar=self.mod.t[:, L, 16 + c, v:v + 1],
                        in1=xt.t[:, c, 0:n], op0=ALU.mult, op1=ALU.add), reads=[k.pb[b], self.mod.b, xt.b],
                        writes=[xt.b])
                steps.append(oproj)
            if kind == 1 and bi + 1 < len(blocks):
                steps.append(lambda: load_ob(bi + 1))
            steps.append(lambda: self.norm_part1(env, blk, None, xt, sq=st["hx"]))
            steps.append(lambda: self.norm_part2(env, blk, xt, st["hx"], L, self.g2t, 24, st["hx"], 0))
            return steps

        def ffn(bi, inter):
            t0, n, v = blocks[bi]
            st = state[bi]
            xt = st["xt"]
            h2 = st["hx"]
            act = actr.next()
            for f in range(FC):
                bg, bu = pj.next(), pj.next()
                k.mm_group(ps[:, bg, 0:n], [(wgu.t[:, kc, f * 128:(f + 1) * 128], h2.t[:, kc, 0:n])
                                            for kc in range(KC)], reads=[wgu.b, h2.b], writes=[k.pb[bg]])
                k.mm_group(ps[:, bu, 0:n], [(wgu.t[:, kc, FH + f * 128:FH + (f + 1) * 128], h2.t[:, kc, 0:n])
                                            for kc in range(KC)], reads=[wgu.b, h2.b], writes=[k.pb[bu]])
                eg, rg = sgr.next(), sgr.next()
                k.op("act", lambda e: e.activation(out=eg.t[:, 0:n], in_=ps[:, bg, 0:n], func=AF.Exp, scale=-1.0),
                     reads=[k.pb[bg]], writes=[eg.b])
                k.op("act", lambda e: e.activation(out=rg.t[:, 0:n], in_=eg.t[:, 0:n], func=AF.Ln, bias=1.0),
                     reads=[eg.b], writes=[rg.b])
                k.op("act", lambda e: e.activation(out=eg.t[:, 0:n], in_=rg.t[:, 0:n], func=AF.Exp, scale=-1.0),
                     reads=[rg.b], writes=[eg.b])
                k.op("dve", lambda e: e.tensor_tensor(out=rg.t[:, 0:n], in0=ps[:, bu, 0:n], in1=eg.t[:, 0:n],
                                                      op=ALU.mult), reads=[k.pb[bu], eg.b], writes=[rg.b])
                k.op("dve", lambda e: e.tensor_tensor(out=act.t[:, f, 0:n], in0=ps[:, bg, 0:n], in1=rg.t[:, 0:n],
                                                      op=ALU.mult), reads=[k.pb[bg], rg.b], writes=[act.b])
                if inter:
                    inter.pop(0)()
            while inter:
                inter.pop(0)()
            for c in range(KC):
                b = pj.next()
                k.mm_group(ps[:, b, 0:n], [(wd.t[:, f, c * 128:(c + 1) * 128], act.t[:, f, 0:n]) for f in range(FC)],
                           reads=[wd.b, act.b], writes=[k.pb[b]])
                k.op("dve", lambda e, b=b, c=c: e.scalar_tensor_tensor(
                    out=xt.t[:, c, 0:n], in0=ps[:, b, 0:n], scalar=self.mod.t[:, L, 40 + c, v:v + 1],
                    in1=xt.t[:, c, 0:n], op0=ALU.mult, op1=ALU.add), reads=[k.pb[b], self.mod.b, xt.b],
                    writes=[xt.b])
            if last and self.final:
                E = k.E["pe"]
                for ti in range(n // 128):
                    fo = ost.next()
                    for half in range(2):
                        b = pj.next()
                        k._wait(E, k._deps([xt.b, self.ident_f.b], [k.pb[b]]))
                        inst = None
                        for cc in range(4):
                            c = half * 4 + cc
                            inst = E.e.transpose(ps[:, b, cc * 128:(cc + 1) * 128],
                                                 xt.t[:, c, ti * 128:(ti + 1) * 128], self.ident_f.t[:])
                        E.cnt += 1
                        inst.then_inc(E.sem, 1)
                        k._record(Ev(E, E.cnt), [xt.b], [k.pb[b]])
                        k.op("act", lambda e, b=b, half=half, fo=fo: e.activation(
                            out=fo.t[:, half * 512:(half + 1) * 512], in_=ps[:, b, 0:512], func=AF.Copy),
                            reads=[k.pb[b]], writes=[fo.b])
                    r0 = t0 - NCTX + ti * 128
                    k.dma("sp", self.OUT[r0:r0 + 128, :], fo.t[:], reads=[fo.b])
            else:
                k.dma("sp", self.XTw.rearrange("(c p) t -> p c t", p=128)[:, :, t0:t0 + n], xt.t[:, :, 0:n],
                      reads=[xt.b])

        load(0)
        load_ob(0)
        for s_ in pre_steps(0):
            s_()
        for bi in range(len(blocks)):
            if bi + 1 < len(blocks):
                load(bi + 1)
            ffn(bi, pre_steps(bi + 1) if bi + 1 < len(blocks) else [])
        k.phase_end()

    def p1_gla(self, L):
        k, W = self.k, self.W
        ps = k.ps
        nc = self.nc
        if not hasattr(self, "GK"):
            self.GK = nc.dram_tensor("GK", [T, 512], BF16).ap()
            self.GV = nc.dram_tensor("GV", [T, 1024], BF16).ap()
            self.GG = nc.dram_tensor("GG", [2, T, 512], F32).ap()
            self.OT2 = nc.dram_tensor("OT2", [D, T], BF16).ap()
        k.phase_begin()
        win = k.tb("win", [128, KC, 3072], BF16, dma=True)
        for c in range(KC):
            k.dma("pool", win.t[:, c, :], W["gla_w_in"][0][c * 128:(c + 1) * 128, :], writes=[win.b])
        w1 = k.tb("w1", [128, KC, 32], BF16, dma=True)
        for d, nm in enumerate(("gla_gate_w1_fwd", "gla_gate_w1_bwd")):
            k.dma("pool", w1.t[:, :, d * 16:(d + 1) * 16], W[nm][0].rearrange("(c p) r -> p c r", p=128),
                  writes=[w1.b], allow_slow_non_contiguous=True)
        w2e = k.tb("w2e", [64, 2, 512], F32, dma=True)
        k.op("dve", lambda e: e.memset(w2e.t[:], 0.0), writes=[w2e.b])
        for d, (nw, nb) in enumerate((("gla_gate_w2_fwd", "gla_gate_b_fwd"), ("gla_gate_w2_bwd", "gla_gate_b_bwd"))):
            k.dma("sp", w2e.t[0:16, d, :], W[nw][0], writes=[w2e.b])
            k.dma("sp", w2e.t[32:33, d, :], W[nb][0].rearrange("(o n) -> o n", o=1), writes=[w2e.b])
        ut = [k.tb("ut%d" % d, [64, 512], F32) for d in range(2)]
        for d in range(2):
            k.op("dve", lambda e, d=d: e.memset(ut[d].t[:], 0.0), writes=[ut[d].b])
            k.op("dve", lambda e, d=d: e.memset(ut[d].t[32:33, :], 1.0), writes=[ut[d].b])
        env = self.norm_env(512)
        xtr = k.ring("xt", 2, [128, KC, 512], F32, dma=True)
        hTr = k.ring("hT", 2, [128, KC, 512], BF16)
        fmo = k.ring("fmo", 3, [128, 512], BF16, dma=True)
        tmo = k.ring("tmo", 3, [128, 512], BF16, dma=True)
        ger = k.ring("ge", 2, [128, 512], F32)
        glr = k.ring("gl", 2, [128, 512], F32)
        ggr = k.ring("gg", 2, [128, 512], F32, dma=True)
        pj = Ring([1, 2, 3, 4, 5, 6, 7])
        blocks = [(0, NCTX, 1)] + [(NCTX + 512 * i, 512, 0) for i in range(SEQ // 512)]
        qscale = 128 ** -0.5
        cp = [0]
        jobs = []
        for bi, blk in enumerate(blocks):
            t0, n, v = blk
            st = {}

            def norm(blk=blk, st=st):
                xt = xtr.next()
                sq = self.norm_part1(env, blk, self.xt_r, xt)
                hT = hTr.next()
                self.norm_part2(env, blk, xt, sq, L, self.g1t, 0, hT, 0)
                st["hT"] = hT
            jobs.append([norm])
            for ci in range(16):
                d = {}
                col0 = ci * 128 if ci < 8 else 2048 + (ci - 8) * 128

                def A(d=d, col0=col0, st=st, n=n):
                    d["b"] = pj.next()
                    k.mm_group(ps[:, d["b"], 0:n], [(win.t[:, c, col0:col0 + 128], st["hT"].t[:, c, 0:n])
                                                    for c in range(KC)], reads=[win.b, st["hT"].b],
                               writes=[k.pb[d["b"]]])

                def B(d=d, ci=ci, n=n, t0=t0):
                    o = fmo.next()
                    if ci < 4:
                        k.op("act", lambda e: e.activation(out=o.t[:, 0:n], in_=ps[:, d["b"], 0:n], func=AF.Copy,
                                                           scale=qscale), reads=[k.pb[d["b"]]], writes=[o.b])
                        dst = self.QT[ci * 128:(ci + 1) * 128, t0:t0 + n]
                    elif ci < 8:
                        k.op("dve", lambda e: e.tensor_copy(out=o.t[:, 0:n], in_=ps[:, d["b"], 0:n]),
                             reads=[k.pb[d["b"]]], writes=[o.b])
                        dst = self.QT[ci * 128:(ci + 1) * 128, t0:t0 + n]
                    else:
                        k.op("act", lambda e: e.activation(out=o.t[:, 0:n], in_=ps[:, d["b"], 0:n], func=AF.Silu),
                             reads=[k.pb[d["b"]]], writes=[o.b])
                        dst = self.KTd[(ci - 8) * 128:(ci - 7) * 128, t0:t0 + n]
                    k.dma("sp", dst, o.t[:, 0:n], reads=[o.b])
                jobs.append([A, B])
            for ti in range(n // 128):
                for cg in range(3):
                    d = {}

                    def A(d=d, ti=ti, cg=cg, st=st):
                        d["b"] = pj.next()
                        c0 = 512 + cg * 512
                        k.mm_group(ps[:, d["b"], 0:512],
                                   [(st["hT"].t[:, c, ti * 128:(ti + 1) * 128], win.t[:, c, c0:c0 + 512])
                                    for c in range(KC)], reads=[win.b, st["hT"].b], writes=[k.pb[d["b"]]])

                    def B(d=d, ti=ti, cg=cg, t0=t0):
                        o = tmo.next()
                        cp[0] += 1
                        if cp[0] % 2:
                            k.op("act", lambda e: e.activation(out=o.t[:], in_=ps[:, d["b"], 0:512], func=AF.Copy),
                                 reads=[k.pb[d["b"]]], writes=[o.b])
                        else:
                            k.op("dve", lambda e: e.tensor_copy(out=o.t[:], in_=ps[:, d["b"], 0:512]),
                                 reads=[k.pb[d["b"]]], writes=[o.b])
                        tt = t0 + ti * 128
                        dst = self.GK[tt:tt + 128, :] if cg == 0 else self.GV[tt:tt + 128, (cg - 1) * 512:cg * 512]
                        k.dma("sp", dst, o.t[:], reads=[o.b])
                    jobs.append([A, B])
            for dd in range(2):
                d = {}

                def A(d=d, dd=dd, st=st, n=n):
                    d["b"] = pj.next()
                    k.mm_group(ps[0:16, d["b"], 0:n], [(w1.t[:, c, dd * 16:(dd + 1) * 16], st["hT"].t[:, c, 0:n])
                                                       for c in range(KC)], reads=[w1.b, st["hT"].b],
                               writes=[k.pb[d["b"]]])

                def B(d=d, dd=dd, n=n):
                    k.op("dve", lambda e: e.tensor_copy(out=ut[dd].t[0:16, 0:n], in_=ps[0:16, d["b"], 0:n]),
                         reads=[k.pb[d["b"]]], writes=[ut[dd].b])
                jobs.append([A, B])
            for ti in range(n // 128):
                for dd in range(2):
                    d = {}

                    def A(d=d, dd=dd, ti=ti):
                        d["b"] = pj.next()
                        k.mm_group(ps[:, d["b"], 0:512],
                                   [(ut[dd].t[0:33, ti * 128:(ti + 1) * 128], w2e.t[0:33, dd, :])],
                                   reads=[ut[dd].b, w2e.b], writes=[k.pb[d["b"]]])

                    def B(d=d, dd=dd, ti=ti, t0=t0):
                        ge, gl, gg = ger.next(), glr.next(), ggr.next()
                        k.op("act", lambda e: e.activation(out=ge.t[:], in_=ps[:, d["b"], 0:512], func=AF.Exp,
                                                           scale=-1.0), reads=[k.pb[d["b"]]], writes=[ge.b])
                        k.op("act", lambda e: e.activation(out=gl.t[:], in_=ge.t[:], func=AF.Ln, bias=1.0),
                             reads=[ge.b], writes=[gl.b])
                        k.op("dve", lambda e: e.tensor_scalar(out=gg.t[:], in0=gl.t[:], scalar1=-1.0 / 16.0,
                                                              scalar2=None, op0=ALU.mult), reads=[gl.b], writes=[gg.b])
                        tt = t0 + ti * 128
                        k.dma("sp", self.GG[dd, tt:tt + 128, :], gg.t[:], reads=[gg.b])
                    jobs.append([A, B])
        run_pipeline(jobs)
        k.phase_end()

    def p2_gla(self, L, last):
        k, W = self.k, self.W
        ps = k.ps
        k.phase_begin()
        NCH = T // 64
        SEGC = 4
        tri = k.tb("tri", [64, 4, 64], F32, dma=True)
        k.dma("sp", tri.t[:], W["c_gla_tri"].rearrange("p (a t) -> p a t", a=4), writes=[tri.b])
        S = k.tb("S", [128, 8, 256], F32)
        Sb = k.tb("Sb", [128, 8, 256], BF16)
        Sbuf = [Buf("S%d" % i) for i in range(8)]
        Sbb = [Buf("Sb%d" % i) for i in range(8)]
        k.op("dve", lambda e: e.memset(S.t[:], 0.0), writes=Sbuf)
        k.op("dve", lambda e: e.memset(Sb.t[:], 0.0), writes=Sbb)
        gr = [k.ring("g%d" % d, 2, [64, SEGC, 512], F32, dma=True) for d in range(2)]
        kkr = [k.ring("kk%d" % d, 2, [64, SEGC, 512], BF16, dma=True) for d in range(2)]
        vvr = [k.ring("vv%d" % d, 2, [64, SEGC, 1024], BF16, dma=True) for d in range(2)]
        qTr = [k.ring("qT%d" % d, 2, [128, 8, SEGC * 64], BF16, dma=True) for d in range(2)]
        osr = [k.ring("os%d" % d, 2, [128, 8, SEGC * 64], BF16, dma=True) for d in range(2)]
        Epr = k.ring("Ep", 16, [128, 64], F32)
        Enr = k.ring("En", 16, [128, 64], F32)
        Err = k.ring("Er", 16, [64, 128], F32)
        qer = k.ring("qe", 16, [128, 64], BF16)
        ker = k.ring("keT", 16, [128, 64], BF16)
        k2r = k.ring("ke2", 16, [64, 128], BF16)
        Amr = k.ring("Am", 16, [64, 64], BF16)
        order = [list(range(NCH)), [3, 2, 1, 0] + list(range(NCH - 1, 3, -1))]
        cur = [None, None]
        osc = [None, None]

        def seg_of(c):
            return c // SEGC

        def load_seg(d, sg):
            t0 = sg * SEGC * 64
            nt = SEGC * 64
            g, kk, vv, qT = gr[d].next(), kkr[d].next(), vvr[d].next(), qTr[d].next()
            k.dma("sp", g.t[:], self.GG[d, t0:t0 + nt, :].rearrange("(c p) f -> p c f", p=64), writes=[g.b])
            k.dma("sp", kk.t[:], self.GK[t0:t0 + nt, :].rearrange("(c p) f -> p c f", p=64), writes=[kk.b])
            k.dma("sp", vv.t[:], self.GV[t0:t0 + nt, :].rearrange("(c p) f -> p c f", p=64), writes=[vv.b])
            k.dma("sp", qT.t[:], self.QT.rearrange("(c p) t -> p c t", p=128)[:, :, t0:t0 + nt], writes=[qT.b])
            return (sg, g, kk, vv, qT)

        def flush_os(d):
            sg, o = osc[d]
            t0 = sg * SEGC * 64
            dst = (self.OT if d == 0 else self.OT2).rearrange("(c p) t -> p c t", p=128)[:, :, t0:t0 + SEGC * 64]
            k.dma("sp", dst, o.t[:], reads=[o.b])

        nxt = [None, None]
        for d in range(2):
            cur[d] = load_seg(d, seg_of(order[d][0]))
        for step in range(NCH):
            units = []
            for d in range(2):
                c = order[d][step]
                sg = seg_of(c)
                if cur[d][0] != sg:
                    cur[d] = nxt[d] if (nxt[d] is not None and nxt[d][0] == sg) else load_seg(d, sg)
                    nxt[d] = None
                if osc[d] is None or osc[d][0] != sg:
                    if osc[d] is not None:
                        flush_os(d)
                    osc[d] = (sg, osr[d].next())
                for hh in range(4):
                    units.append((d, hh, c))
            sts = [dict() for _ in units]
            for u, st in zip(units, sts):
                d, hh, c = u
                sg, g, kk, vv, qT = cur[d]
                ci = c - sg * SEGC
                bnk = d * 4 + hh
                st["bnk"] = bnk
                E = k.E["pe"]
                gsl = g.t[:, ci, hh * 128:(hh + 1) * 128]
                k._wait(E, k._deps([g.b, tri.b], [k.pb[bnk]]))
                E.e.matmul(ps[:, bnk, 0:64], gsl, tri.t[:, d, :], start=True, stop=True)
                inst = E.e.matmul(ps[0:64, bnk, 64:192], tri.t[:, 2 + d, :], gsl, start=True, stop=True)
                E.cnt += 1
                inst.then_inc(E.sem, 1)
                k._record(Ev(E, E.cnt), [g.b, tri.b], [k.pb[bnk]])
                Ep, En, Er = Epr.next(), Enr.next(), Err.next()
                qe, keT, ke2 = qer.next(), ker.next(), k2r.next()
                st.update(Ep=Ep, qe=qe, keT=keT, ke2=ke2)
                k.op("act", lambda e: e.activation(out=Ep.t[:], in_=ps[:, bnk, 0:64], func=AF.Exp),
                     reads=[k.pb[bnk]], writes=[Ep.b])
                k.op("act", lambda e: e.activation(out=En.t[:], in_=ps[:, bnk, 0:64], func=AF.Exp, scale=-1.0),
                     reads=[k.pb[bnk]], writes=[En.b])
                k.op("act", lambda e: e.activation(out=Er.t[:], in_=ps[0:64, bnk, 64:192], func=AF.Exp),
                     reads=[k.pb[bnk]], writes=[Er.b])
                cs = slice(ci * 64, (ci + 1) * 64)
                k.op("dve", lambda e: e.tensor_tensor(out=qe.t[:], in0=qT.t[:, hh, cs], in1=Ep.t[:], op=ALU.mult),
                     reads=[qT.b, Ep.b], writes=[qe.b])
                k.op("dve", lambda e: e.tensor_tensor(out=keT.t[:], in0=qT.t[:, 4 + hh, cs], in1=En.t[:],
                                                      op=ALU.mult), reads=[qT.b, En.b], writes=[keT.b])
                k.op("dve", lambda e: e.tensor_tensor(out=ke2.t[:], in0=kk.t[:, ci, hh * 128:(hh + 1) * 128],
                                                      in1=Er.t[:], op=ALU.mult), reads=[kk.b, Er.b], writes=[ke2.b])
            for u, st in zip(units, sts):
                d, hh, c = u
                bnk = st["bnk"]
                k.mm_group(ps[0:64, bnk, 192:256], [(st["keT"].t[:], st["qe"].t[:])],
                           reads=[st["keT"].b, st["qe"].b], writes=[k.pb[bnk]])
                Am = Amr.next()
                st["Am"] = Am
                k.op("dve", lambda e: e.tensor_tensor(out=Am.t[:], in0=ps[0:64, bnk, 192:256], in1=tri.t[:, d, :],
                                                      op=ALU.mult), reads=[k.pb[bnk], tri.b], writes=[Am.b])
            for u, st in zip(units, sts):
                d, hh, c = u
                sg, g, kk, vv, qT = cur[d]
                ci = c - sg * SEGC
                bnk = st["bnk"]
                ch = d * 4 + hh
                E = k.E["pe"]
                k._wait(E, k._deps([Sbb[ch], st["qe"].b, vv.b, st["Am"].b, st["ke2"].b], [k.pb[bnk]]))
                for e2 in range(2):
                    E.e.matmul(ps[:, bnk, 256 + e2 * 64:256 + (e2 + 1) * 64], Sb.t[:, ch, e2 * 128:(e2 + 1) * 128],
                               st["qe"].t[:], start=True, stop=False)
                    inst = E.e.matmul(ps[:, bnk, 256 + e2 * 64:256 + (e2 + 1) * 64],
                                      vv.t[:, ci, hh * 256 + e2 * 128:hh * 256 + (e2 + 1) * 128], st["Am"].t[:],
                                      start=False, stop=True)
                E.cnt += 1
                inst.then_inc(E.sem, 1)
                k._record(Ev(E, E.cnt), [Sbb[ch], st["qe"].b, vv.b, st["Am"].b], [k.pb[bnk]])
                sgo, o = osc[d]
                k.op("act", lambda e: e.activation(
                    out=o.t[:, 2 * hh:2 * hh + 2, ci * 64:(ci + 1) * 64],
                    in_=ps[:, bnk, 256:384].rearrange("p (a t) -> p a t", a=2), func=AF.Copy),
                    reads=[k.pb[bnk]], writes=[o.b])
                k.mm_group(ps[:, bnk, 0:256], [(st["ke2"].t[:], vv.t[:, ci, hh * 256:(hh + 1) * 256])],
                           reads=[st["ke2"].b, vv.b], writes=[k.pb[bnk]])
                dcol = 63 if d == 0 else 0
                k.op("dve", lambda e: e.scalar_tensor_tensor(
                    out=S.t[:, ch, :], in0=S.t[:, ch, :], scalar=st["Ep"].t[:, dcol:dcol + 1], in1=ps[:, bnk, 0:256],
                    op0=ALU.mult, op1=ALU.add), reads=[Sbuf[ch], st["Ep"].b, k.pb[bnk]], writes=[Sbuf[ch]])
                k.op("pool", lambda e: e.tensor_copy(out=Sb.t[:, ch, :], in_=S.t[:, ch, :]), reads=[Sbuf[ch]],
                     writes=[Sbb[ch]])
            if step + 1 < NCH:
                for d in range(2):
                    c2 = order[d][step + 1]
                    for la in range(step + 1, min(step + 1 + SEGC, NCH)):
                        sg2 = seg_of(order[d][la])
                        if sg2 != cur[d][0]:
                            if nxt[d] is None:
                                nxt[d] = load_seg(d, sg2)
                            break
        for d in range(2):
            flush_os(d)
        k.phase_end()

    def gla_mixer_out(self, L, blk, on, env, pj, tl, hh_only):
        k = self.k
        ps = k.ps
        t0, n, v = blk
        for hh in (hh_only,):
            of, ob, rt = tl["of"].next(), tl["ob"].next(), tl["rt"].next()
            rows = slice(hh * 256, (hh + 1) * 256)
            k.dma("sp", of.t[:, :, 0:n], self.OT[rows, t0:t0 + n].rearrange("(c p) t -> p c t", p=128), writes=[of.b])
            k.dma("sp", ob.t[:, :, 0:n], self.OT2[rows, t0:t0 + n].rearrange("(c p) t -> p c t", p=128), writes=[ob.b])
            k.dma("sp", rt.t[:, :, 0:n], self.KTd[rows, t0:t0 + n].rearrange("(c p) t -> p c t", p=128), writes=[rt.b])
            ss = []
            b = pj.next()
            for e2 in range(2):
                s_, sq = tl["s"].next(), tl["sq"].next()
                ss.append(s_)
                k.op("dve", lambda e, e2=e2, s_=s_: e.tensor_tensor(out=s_.t[:, 0:n], in0=of.t[:, e2, 0:n],
                                                                    in1=ob.t[:, e2, 0:n], op=ALU.add),
                     reads=[of.b, ob.b], writes=[s_.b])
                k.op("pool", lambda e, s_=s_, sq=sq: e.tensor_tensor(out=sq.t[:, 0:n], in0=s_.t[:, 0:n],
                                                                     in1=s_.t[:, 0:n], op=ALU.mult),
                     reads=[s_.b], writes=[sq.b])
                k.mm_group(ps[:, b, 0:n], [(self.ones_b.t[:], sq.t[:, 0:n])], reads=[self.ones_b.b, sq.b],
                           writes=[k.pb[b]], first=(e2 == 0), last=(e2 == 1))
            ln, rs = tl["ln"].next(), tl["rs"].next()
            k.op("act", lambda e: e.activation(out=ln.t[:, 0:n], in_=ps[:, b, 0:n], func=AF.Ln, scale=1.0 / 256,
                                               bias=EPS), reads=[k.pb[b]], writes=[ln.b])
            k.op("act", lambda e: e.activation(out=rs.t[:, 0:n], in_=ln.t[:, 0:n], func=AF.Exp, scale=-0.5),
                 reads=[ln.b], writes=[rs.b])
            for e2 in range(2):
                s_ = ss[e2]
                k.op("dve", lambda e, e2=e2, s_=s_: e.scalar_tensor_tensor(
                    out=s_.t[:, 0:n], in0=s_.t[:, 0:n], scalar=tl["gain"].t[:, e2:e2 + 1], in1=rs.t[:, 0:n],
                    op0=ALU.mult, op1=ALU.mult), reads=[s_.b, tl["gain"].b, rs.b], writes=[s_.b])
                k.op("dve", lambda e, e2=e2, s_=s_: e.tensor_tensor(
                    out=on.t[:, 2 * hh + e2, 0:n], in0=s_.t[:, 0:n], in1=rt.t[:, e2, 0:n], op=ALU.mult),
                    reads=[s_.b, rt.b], writes=[on.b])


WEIGHT_SHAPES = {
    "ada_w": (4, 1024, 6144), "ada_b": (4, 6144), "norm1_g": (4, 1024), "norm2_g": (4, 1024),
    "ffn_w_gu": (4, 1024, 5632), "ffn_w_down": (4, 2816, 1024),
    "diff_w_in": (2, 1024, 3072), "diff_w_out": (2, 1024, 1024), "diff_q_norm": (2, 64), "diff_k_norm": (2, 64),
    "diff_lambda_q1": (2, 64), "diff_lambda_k1": (2, 64), "diff_lambda_q2": (2, 64), "diff_lambda_k2": (2, 64),
    "diff_subln": (2, 128),
    "gqa_w_in": (1, 1024, 1536), "gqa_w_out": (1, 1024, 1024), "gqa_q_norm": (1, 128), "gqa_k_norm": (1, 128),
    "gla_w_in": (1, 1024, 3072), "gla_gate_w1_fwd": (1, 1024, 16), "gla_gate_w2_fwd": (1, 16, 512),
    "gla_gate_b_fwd": (1, 512), "gla_gate_w1_bwd": (1, 1024, 16), "gla_gate_w2_bwd": (1, 16, 512),
    "gla_gate_b_bwd": (1, 512), "gla_out_norm": (1, 256), "gla_w_out": (1, 1024, 1024),
}
CONST_SHAPES = {
    "c_ident": (128, 128), "c_bd64": (128, 128), "c_r64t": (128, 128), "c_r128t": (128, 128),
    "c_cos64": (128, T), "c_sin64": (128, T), "c_cos128": (128, T), "c_sin128": (128, T),
    "c_gla_tri": (64, 256),
}


def _needed(layers):
    need = {"ada_w", "ada_b", "norm1_g", "norm2_g", "ffn_w_gu", "ffn_w_down", "c_ident", "c_bd64", "c_r64t",
            "c_r128t"}
    for L in layers:
        kind = L % 3
        if kind == 0:
            need |= {n for n in WEIGHT_SHAPES if n.startswith("diff_")} | {"c_cos64", "c_sin64"}
        elif kind == 1:
            need |= {n for n in WEIGHT_SHAPES if n.startswith("gqa_")} | {"c_cos128", "c_sin128"}
        else:
            need |= {n for n in WEIGHT_SHAPES if n.startswith("gla_")} | {n for n in CONST_SHAPES if n.startswith("c_gla")}
    return need


def _rope_tables(dh):
    half = dh // 2
    inv = (10000.0 ** (-np.arange(0, half, 2, dtype=np.float32) / np.float32(half))).astype(np.float32)
    t = np.arange(SEQ)
    row = (t // GRID_W).astype(np.float32)
    col = (t % GRID_W).astype(np.float32)

    def ax(pos):
        a = pos[:, None] * inv[None, :]
        return np.concatenate([a, a], axis=-1)

    ang = np.concatenate([ax(row), ax(col)], axis=-1).astype(np.float32)
    cos = np.concatenate([np.ones((NCTX, dh), np.float32), np.cos(ang)], axis=0)
    sin = np.concatenate([np.zeros((NCTX, dh), np.float32), np.sin(ang)], axis=0)
    rep = 128 // dh
    cosT = np.ascontiguousarray(np.tile(cos.T, (rep, 1))).astype(np.float32)
    sinT = np.ascontiguousarray(np.tile(sin.T, (rep, 1))).astype(np.float32)
    return cosT, sinT


def _rot_t(dh):
    q = dh // 4
    R = np.zeros((128, 128), np.float32)
    for h0 in range(0, 128, dh):
        for half in range(2):
            base = h0 + half * 2 * q
            for i in range(q):
                R[base + q + i, base + i] = -1.0
                R[base + i, base + q + i] = 1.0
    return R


def _consts():
    c = {}
    c["c_ident"] = np.eye(128, dtype=np.float32)
    bd = np.zeros((128, 128), np.float32)
    bd[0:64, 0:64] = 1.0
    bd[64:128, 64:128] = 1.0
    c["c_bd64"] = bd
    c["c_r64t"] = _rot_t(64)
    c["c_r128t"] = _rot_t(128)
    c["c_cos64"], c["c_sin64"] = _rope_tables(64)
    c["c_cos128"], c["c_sin128"] = _rope_tables(128)
    si = np.arange(64)[:, None]
    ti = np.arange(64)[None, :]
    tri = np.stack([(si <= ti), (si >= ti), (si > ti), (si < ti)], axis=1).astype(np.float32)
    c["c_gla_tri"] = np.ascontiguousarray(tri.reshape(64, 256))
    return c


_PROG_CACHE = {}


def _get_prog(layers, first, final):
    key = (tuple(layers), first, final)
    if key not in _PROG_CACHE:
        _PROG_CACHE[key] = Prog(list(layers), first, final)
    return _PROG_CACHE[key]


LAUNCH_PLAN = [[0, 1, 2, 3]]


def kernel(**inputs):
    n = 8
    consts = _consts()
    shared = {nm: np.ascontiguousarray(inputs[nm], dtype=np.float32) for nm in WEIGHT_SHAPES}
    shared.update(consts)
    shared["c_ctx"] = np.ascontiguousarray(inputs["c_ctx"], dtype=np.float32)
    xt = None
    out = None
    for li, layers in enumerate(LAUNCH_PLAN):
        first = li == 0
        final = li == len(LAUNCH_PLAN) - 1
        prog = _get_prog(layers, first, final)
        in_maps = []
        for b in range(n):
            m = dict(shared)
            m["c"] = np.ascontiguousarray(inputs["c"][b], dtype=np.float32)
            if first:
                m["x"] = np.ascontiguousarray(inputs["x"][b], dtype=np.float32)
                m["ctx"] = np.ascontiguousarray(inputs["ctx"][b], dtype=np.float32)
            else:
                m["xt_in"] = xt[b]
            m = {kk: vv for kk, vv in m.items() if kk in prog.inputs}
            in_maps.append(m)
        res = run_bass_kernel_spmd(prog.nc, in_maps, core_ids=list(range(n)))
        if final:
            out = np.stack([np.asarray(r["out"]) for r in res.results], axis=0)
        else:
            xt = [np.asarray(r["xt_out"]) for r in res.results]
    return out.astype(np.float32)
```
